# Optimizing a Trainium2 kernel written in Bass

```python
import jax, jax.numpy as jnp
from jax import lax
import numpy as np

D_MODEL = 1024
BATCH = 2
SEQ = 8192
DEPTH = 2

GRID_W = 64
CTX_LEN = 256

MIX_HALF = D_MODEL // 2
RET_HEADS = 4
RET_DV = MIX_HALF // RET_HEADS
RET_DK = RET_DV // 2
RET_DECAY_BASE = 5
GLA_HEADS = 4
GLA_DV = MIX_HALF // GLA_HEADS
GLA_DK = GLA_DV // 2
GLA_GATE_RANK = 16
GLA_GATE_TAU = 16.0
SCAN_CHUNK = 64
RET_QK = RET_HEADS * RET_DK
RET_V = RET_HEADS * RET_DV
GLA_QK = GLA_HEADS * GLA_DK
GLA_V = GLA_HEADS * GLA_DV
EVEN_SPLITS = (RET_QK, RET_QK, RET_V, RET_V, GLA_QK, GLA_QK, GLA_V, GLA_V, 2 * GLA_GATE_RANK)
EVEN_PROJ = 2 * RET_QK + 2 * RET_V + 2 * GLA_QK + 2 * GLA_V + 2 * GLA_GATE_RANK

ATT_HEAD_DIM = 64
ATT_Q_HEADS = D_MODEL // ATT_HEAD_DIM
ATT_GROUP = 4
ATT_KV_HEADS = ATT_Q_HEADS // ATT_GROUP
WINDOW = 128
ATT_BLOCK = 128
ATT_QW = ATT_Q_HEADS * ATT_HEAD_DIM
ATT_KVW = ATT_KV_HEADS * ATT_HEAD_DIM
ATT_PROJ = ATT_QW + 2 * ATT_KVW
ROPE_BASE = 10000.0

D_FF = 4 * D_MODEL
DEEPNORM_ALPHA = (2.0 * DEPTH) ** 0.25
DEEPNORM_BETA = (8.0 * DEPTH) ** -0.25
LN_EPS = 1e-5
RMS_EPS = 1e-6
N_EVEN = (DEPTH + 1) // 2
N_ODD = DEPTH // 2

kernel_name = 'hybrid_retention_gla_swa_dit'


def layer_norm(x, g, b):
    xf = x.astype(jnp.float32)
    mu = jnp.mean(xf, axis=-1, keepdims=True)
    var = jnp.mean(jnp.square(xf - mu), axis=-1, keepdims=True)
    return ((xf - mu) * lax.rsqrt(var + LN_EPS) * g + b).astype(x.dtype)


def rms_norm(x, g=None):
    xf = x.astype(jnp.float32)
    y = xf * lax.rsqrt(jnp.mean(jnp.square(xf), axis=-1, keepdims=True) + RMS_EPS)
    if g is not None:
        y = y * g
    return y.astype(x.dtype)


def modulation(cvec, w, b):
    m = jax.nn.silu(cvec) @ w + b
    return jnp.split(m[..., None, :], 6, axis=-1)


def sublayer_residual(x, y, gate, g, b):
    return layer_norm(DEEPNORM_ALPHA * x + gate * y, g, b)


def split_cols(t, sizes):
    idx = np.cumsum(sizes)[:-1].tolist()
    return jnp.split(t, idx, axis=-1)


def split_heads(t, h):
    b, n, _ = t.shape
    return t.reshape(b, n, h, -1).transpose(0, 2, 1, 3)


def merge_heads(t):
    b, h, n, d = t.shape
    return t.transpose(0, 2, 1, 3).reshape(b, n, h * d)


def sq_relu_mlp(h, w1, w2):
    return jnp.square(jax.nn.relu(h @ w1)) @ w2


def chunk_gated_scan(q, k, v, log_a, s0, strict):
    f32 = jnp.float32
    B, H, T, dk = q.shape
    dv = v.shape[-1]
    n = T // SCAN_CHUNK
    qc = q.astype(f32).reshape(B, H, n, SCAN_CHUNK, dk)
    kc = k.astype(f32).reshape(B, H, n, SCAN_CHUNK, dk)
    vc = v.astype(f32).reshape(B, H, n, SCAN_CHUNK, dv)
    bcum = jnp.cumsum(log_a.astype(f32).reshape(B, H, n, SCAN_CHUNK, dk), axis=3)
    b_last = bcum[:, :, :, -1:, :]
    q_dec = qc * jnp.exp(bcum)
    k_inv = kc * jnp.exp(-bcum)
    k_end = kc * jnp.exp(b_last - bcum)
    mask = jnp.tril(jnp.ones((SCAN_CHUNK, SCAN_CHUNK), bool), k=-1 if strict else 0)
    scores = jnp.einsum('bhncd,bhnsd->bhncs', q_dec, k_inv)
    o_intra = jnp.einsum('bhncs,bhnse->bhnce', jnp.where(mask, scores, 0.0), vc)
    kv = jnp.einsum('bhnsd,bhnse->bhnde', k_end, vc)
    chunk_decay = jnp.exp(b_last[:, :, :, 0, :])

    def step(s, inp):
        kv_n, dec_n = inp
        return dec_n[..., None] * s + kv_n, s

    s_final, s_prev = lax.scan(step, s0.astype(f32),
                               (jnp.moveaxis(kv, 2, 0), jnp.moveaxis(chunk_decay, 2, 0)))
    s_prev = jnp.moveaxis(s_prev, 0, 2)
    o_inter = jnp.einsum('bhncd,bhnde->bhnce', q_dec, s_prev)
    o = (o_intra + o_inter).reshape(B, H, T, dv)
    return o.astype(v.dtype), s_final


def bidir_scan(q, k, v, log_a_f, log_a_b, s0_f, s0_b):
    flip = lambda t: jnp.flip(t, axis=2)
    o_f, s_f = chunk_gated_scan(q, k, v, log_a_f, s0_f, strict=False)
    o_b, s_b = chunk_gated_scan(flip(q), flip(k), flip(v), flip(log_a_b), s0_b, strict=True)
    return o_f + flip(o_b), s_f, s_b


def retention_gla_mixer(h, w_in, ret_theta, gla_gk_w, gla_gk_b, gla_norm_g, s0):
    f32 = jnp.float32
    qa, ka, va, ga, qb, kb, vb, gb, lr = split_cols(h @ w_in, EVEN_SPLITS)
    qa = split_heads(qa, RET_HEADS)
    ka = split_heads(ka, RET_HEADS) * RET_DK ** -0.5
    va = split_heads(va, RET_HEADS)
    log_gamma = jnp.log1p(-jnp.exp(ret_theta.astype(f32)))
    la_f = jnp.broadcast_to(log_gamma[0][None, :, None, None], qa.shape)
    la_b = jnp.broadcast_to(log_gamma[1][None, :, None, None], qa.shape)
    o_a, ra_f, ra_b = bidir_scan(qa, ka, va, la_f, la_b, s0[0], s0[1])
    y_a = jax.nn.silu(ga) * merge_heads(rms_norm(o_a))
    qb = split_heads(qb, GLA_HEADS) * GLA_DK ** -0.5
    kb = split_heads(kb, GLA_HEADS)
    vb = split_heads(vb, GLA_HEADS)
    lr_f, lr_b = jnp.split(lr, 2, axis=-1)

    def gate(u, w, b):
        z = (u @ w + b).astype(f32)
        return split_heads(jax.nn.log_sigmoid(z) / GLA_GATE_TAU, GLA_HEADS)

    o_b, sb_f, sb_b = bidir_scan(qb, kb, vb, gate(lr_f, gla_gk_w[0], gla_gk_b[0]),
                                 gate(lr_b, gla_gk_w[1], gla_gk_b[1]), s0[2], s0[3])
    y_b = jax.nn.silu(gb) * merge_heads(rms_norm(o_b, gla_norm_g))
    return jnp.concatenate([y_a, y_b], axis=-1), (ra_f, ra_b, sb_f, sb_b)


def axial_rope(t, row, col):
    f32 = jnp.float32
    half = ATT_HEAD_DIM // 2
    inv_freq = ROPE_BASE ** (-jnp.arange(0, half, 2, dtype=f32) / half)

    def rot(u, p):
        ang = p.astype(f32)[:, None] * inv_freq[None, :]
        cos, sin = jnp.cos(ang), jnp.sin(ang)
        u1, u2 = u[..., :half // 2], u[..., half // 2:]
        return jnp.concatenate([u1 * cos - u2 * sin, u1 * sin + u2 * cos], axis=-1)

    return jnp.concatenate([rot(t[..., :half], row), rot(t[..., half:], col)], axis=-1).astype(t.dtype)


def window_attention(q, k, v, k_c, v_c, sink):
    f32 = jnp.float32
    B, Hq, T, dh = q.shape
    nb = T // ATT_BLOCK
    scale = dh ** -0.5
    qb = q.reshape(B, ATT_KV_HEADS, ATT_GROUP, nb, ATT_BLOCK, dh)
    pad = ((0, 0), (0, 0), (ATT_BLOCK, ATT_BLOCK), (0, 0))
    kp = jnp.pad(k, pad).reshape(B, ATT_KV_HEADS, nb + 2, ATT_BLOCK, dh)
    vp = jnp.pad(v, pad).reshape(B, ATT_KV_HEADS, nb + 2, ATT_BLOCK, dh)
    kb = jnp.concatenate([kp[:, :, 0:nb], kp[:, :, 1:nb + 1], kp[:, :, 2:nb + 2]], axis=3)
    vb = jnp.concatenate([vp[:, :, 0:nb], vp[:, :, 1:nb + 1], vp[:, :, 2:nb + 2]], axis=3)
    qi = jnp.arange(ATT_BLOCK)[:, None]
    kj = jnp.arange(3 * ATT_BLOCK)[None, :]
    in_window = jnp.abs(kj - ATT_BLOCK - qi) <= WINDOW
    kpos = (jnp.arange(nb)[:, None] - 1) * ATT_BLOCK + jnp.arange(3 * ATT_BLOCK)[None, :]
    in_range = (kpos >= 0) & (kpos < T)
    valid = in_window[None] & in_range[:, None, :]
    s_loc = jnp.einsum('bkgnqd,bkncd->bkgnqc', qb, kb).astype(f32) * scale
    s_loc = jnp.where(valid, s_loc, -jnp.inf)
    s_ctx = jnp.einsum('bkgnqd,bkld->bkgnql', qb, k_c).astype(f32) * scale
    s_sink = sink.astype(f32).reshape(1, ATT_KV_HEADS, ATT_GROUP, 1, 1, 1)
    m = jnp.maximum(jnp.maximum(jnp.max(s_loc, -1, keepdims=True), jnp.max(s_ctx, -1, keepdims=True)), s_sink)
    e_loc = jnp.exp(s_loc - m)
    e_ctx = jnp.exp(s_ctx - m)
    denom = jnp.sum(e_loc, -1, keepdims=True) + jnp.sum(e_ctx, -1, keepdims=True) + jnp.exp(s_sink - m)
    o = (jnp.einsum('bkgnqc,bkncd->bkgnqd', e_loc, vb.astype(f32))
         + jnp.einsum('bkgnql,bkld->bkgnqd', e_ctx, v_c.astype(f32))) / denom
    return o.reshape(B, Hq, T, dh).astype(q.dtype)


def context_attention(q_c, k_c, v_c, sink):
    f32 = jnp.float32
    B, Hq, L, dh = q_c.shape
    qg = q_c.reshape(B, ATT_KV_HEADS, ATT_GROUP, L, dh)
    s = jnp.einsum('bkgqd,bkld->bkgql', qg, k_c).astype(f32) * dh ** -0.5
    s_sink = jnp.broadcast_to(sink.astype(f32).reshape(1, ATT_KV_HEADS, ATT_GROUP, 1, 1), s.shape[:-1] + (1,))
    p = jax.nn.softmax(jnp.concatenate([s, s_sink], axis=-1), axis=-1)[..., :L]
    return jnp.einsum('bkgql,bkld->bkgqd', p, v_c.astype(f32)).reshape(B, Hq, L, dh).astype(q_c.dtype)


def setup_inputs(seed: int = 0) -> dict:
    key = jax.random.key(seed)
    ks = jax.random.split(key, 19)
    f32 = jnp.float32
    D = D_MODEL

    def nrm(k, shape, fan_in, scale=1.0):
        return jax.random.normal(k, shape, f32) * (scale * fan_in ** -0.5)

    ret_init = -(RET_DECAY_BASE + jnp.arange(RET_HEADS, dtype=f32)) * jnp.log(2.0)
    return {
        'x': jax.random.normal(ks[0], (BATCH, SEQ, D), f32),
        'c': jax.random.normal(ks[1], (BATCH, D), f32),
        'ctx': jax.random.normal(ks[2], (BATCH, CTX_LEN, D), f32),
        'c_ctx': jax.random.normal(ks[3], (D,), f32),
        'w_mod': nrm(ks[4], (DEPTH, D, 6 * D), D),
        'b_mod': 0.02 * jax.random.normal(ks[5], (DEPTH, 6 * D), f32),
        'ln_g': 1.0 + 0.02 * jax.random.normal(ks[6], (DEPTH, 2, D), f32),
        'ln_b': 0.02 * jax.random.normal(ks[7], (DEPTH, 2, D), f32),
        'mlp_w1': nrm(ks[8], (DEPTH, D, D_FF), D),
        'mlp_w2': nrm(ks[9], (DEPTH, D_FF, D), D_FF, DEEPNORM_BETA),
        'ev_w_in': nrm(ks[10], (N_EVEN, D, EVEN_PROJ), D),
        'ev_ret_theta': ret_init + 0.05 * jax.random.normal(ks[11], (N_EVEN, 2, RET_HEADS), f32),
        'ev_gla_gk_w': nrm(ks[12], (N_EVEN, 2, GLA_GATE_RANK, GLA_QK), GLA_GATE_RANK),
        'ev_gla_gk_b': 0.02 * jax.random.normal(ks[13], (N_EVEN, 2, GLA_QK), f32),
        'ev_gla_norm_g': 1.0 + 0.02 * jax.random.normal(ks[14], (N_EVEN, GLA_DV), f32),
        'ev_w_out': nrm(ks[15], (N_EVEN, D, D), D, DEEPNORM_BETA),
        'od_w_qkv': nrm(ks[16], (N_ODD, D, ATT_PROJ), D),
        'od_sink': 0.5 * jax.random.normal(ks[17], (N_ODD, ATT_Q_HEADS), f32),
        'od_w_out': nrm(ks[18], (N_ODD, D, D), D, DEEPNORM_BETA),
    }


def reference(x, c, ctx, c_ctx, w_mod, b_mod, ln_g, ln_b, mlp_w1, mlp_w2,
              ev_w_in, ev_ret_theta, ev_gla_gk_w, ev_gla_gk_b, ev_gla_norm_g, ev_w_out,
              od_w_qkv, od_sink, od_w_out):
    B, T, _ = x.shape
    rows = T // GRID_W
    row = jnp.repeat(jnp.arange(rows), GRID_W)
    col = jnp.tile(jnp.arange(GRID_W), rows)
    xc = ctx
    for i in range(DEPTH):
        last = i == DEPTH - 1
        j = i // 2
        sh1, sc1, g1, sh2, sc2, g2 = modulation(c, w_mod[i], b_mod[i])
        csh1, csc1, cg1, csh2, csc2, cg2 = modulation(c_ctx, w_mod[i], b_mod[i])
        h = x * (1.0 + sc1) + sh1
        hc = xc * (1.0 + csc1) + csh1
        if i % 2 == 0:
            prm = (ev_w_in[j], ev_ret_theta[j], ev_gla_gk_w[j], ev_gla_gk_b[j], ev_gla_norm_g[j])
            s0 = (jnp.zeros((B, RET_HEADS, RET_DK, RET_DV), jnp.float32),
                  jnp.zeros((B, RET_HEADS, RET_DK, RET_DV), jnp.float32),
                  jnp.zeros((B, GLA_HEADS, GLA_DK, GLA_DV), jnp.float32),
                  jnp.zeros((B, GLA_HEADS, GLA_DK, GLA_DV), jnp.float32))
            yc, ctx_states = retention_gla_mixer(hc, *prm, s0)
            y, _ = retention_gla_mixer(h, *prm, ctx_states)
            y = y @ ev_w_out[j]
            if not last:
                yc = yc @ ev_w_out[j]
        else:
            w = od_w_qkv[j]
            q, k, v = split_cols(h @ w, (ATT_QW, ATT_KVW, ATT_KVW))
            q = axial_rope(split_heads(q, ATT_Q_HEADS), row, col)
            k = axial_rope(split_heads(k, ATT_KV_HEADS), row, col)
            v = split_heads(v, ATT_KV_HEADS)
            k_c, v_c = split_cols(hc @ w[:, ATT_QW:], (ATT_KVW, ATT_KVW))
            k_c = split_heads(k_c, ATT_KV_HEADS)
            v_c = split_heads(v_c, ATT_KV_HEADS)
            y = merge_heads(window_attention(q, k, v, k_c, v_c, od_sink[j])) @ od_w_out[j]
            if not last:
                q_c = split_heads(hc @ w[:, :ATT_QW], ATT_Q_HEADS)
                yc = merge_heads(context_attention(q_c, k_c, v_c, od_sink[j])) @ od_w_out[j]
        x = sublayer_residual(x, y, g1, ln_g[i, 0], ln_b[i, 0])
        x = sublayer_residual(x, sq_relu_mlp(x * (1.0 + sc2) + sh2, mlp_w1[i], mlp_w2[i]), g2, ln_g[i, 1], ln_b[i, 1])
        if not last:
            xc = sublayer_residual(xc, yc, cg1, ln_g[i, 0], ln_b[i, 0])
            xc = sublayer_residual(xc, sq_relu_mlp(xc * (1.0 + csc2) + csh2, mlp_w1[i], mlp_w2[i]), cg2, ln_g[i, 1], ln_b[i, 1])
    return x
```

```python
import numpy as np
from contextlib import ExitStack
import concourse.bass as bass
import concourse.mybir as mybir
from concourse.bass_utils import run_bass_kernel_spmd

F32 = mybir.dt.float32
BF16 = mybir.dt.bfloat16
AF = mybir.ActivationFunctionType
ALU = mybir.AluOpType

D = 1024
NB = 18
NCB = 2
NSB = NB + NCB
NOB = 46
WIN = NB * 128
ALPHA = float((2.0 * 2) ** 0.25)
LN_EPS = 1e-5
RMS_EPS = 1e-6


class Buf:
    __slots__ = ("name", "w", "r", "psum")

    def __init__(self, name, psum=False):
        self.name = name
        self.w = None
        self.r = []
        self.psum = psum


class Tracker:
    ENGS = ("pe", "act", "dve", "pool", "sp")

    def __init__(self, nc, ndma_sems=8, same_engine_waits=True):
        self.nc = nc
        self.same = same_engine_waits
        self.prog = {e: [] for e in self.ENGS}
        self.cnt = {e: 0 for e in self.ENGS}
        self.waited = {e: {} for e in self.ENGS}
        self.ndma = ndma_sems
        self.dma_cnt = {}
        self.dma_rr = {e: 0 for e in self.ENGS}
        self.semkeys = list(self.ENGS)
        for e in ("sp", "pool", "act"):
            for i in range(ndma_sems):
                k = f"d_{e}_{i}"
                self.semkeys.append(k)
                self.dma_cnt[k] = 0
        self.sems = {}
        self.ninstr = 0

    def _deps(self, eng, reads, writes):
        need = {}
        cur = self.cnt[eng]

        def add(ev, is_raw):
            if ev is None:
                return
            k, v = ev
            if k == eng:
                if eng == "pe" or not self.same:
                    return
            if need.get(k, 0) < v:
                need[k] = v
        for b in reads:
            add(b.w, True)
            if b.psum:
                for ev in b.r:
                    if ev[0] != eng:
                        add(ev, False)
        for b in writes:
            add(b.w, False)
            for ev in b.r:
                add(ev, False)
        waits = []
        wd = self.waited[eng]
        for k, v in need.items():
            if wd.get(k, 0) < v:
                wd[k] = v
                waits.append((k, v))
        return waits

    def _commit(self, ev, reads, writes):
        for b in reads:
            b.r.append(ev)
            if len(b.r) > 16:
                m = {}
                for k, v in b.r:
                    if m.get(k, 0) < v:
                        m[k] = v
                b.r = list(m.items())
        for b in writes:
            b.w = ev
            b.r = []

    def op(self, eng, fn, reads=(), writes=()):
        waits = self._deps(eng, reads, writes)
        self.cnt[eng] += 1
        ev = (eng, self.cnt[eng])
        self.prog[eng].append((waits, fn, (eng, 1)))
        self._commit(ev, reads, writes)
        self.ninstr += 1
        return ev

    def dma(self, eng, fn, reads=(), writes=()):
        i = self.dma_rr[eng]
        self.dma_rr[eng] = (i + 1) % self.ndma
        k = f"d_{eng}_{i}"
        waits = self._deps(eng, reads, writes)
        prev = self.dma_cnt[k]
        wd = self.waited[eng]
        if prev > 0 and wd.get(k, 0) < prev * 16:
            wd[k] = prev * 16
            waits.append((k, prev * 16))
        self.dma_cnt[k] = prev + 1
        ev = (k, (prev + 1) * 16)
        self.prog[eng].append((waits, fn, (k, 16)))
        self._commit(ev, reads, writes)
        self.ninstr += 1
        return ev

    def barrier(self):
        evs = [(e, self.cnt[e]) for e in self.ENGS if self.cnt[e] > 0]
        evs += [(k, c * 16) for k, c in self.dma_cnt.items() if c > 0]
        for e in self.ENGS:
            self.wait_all_on(e, [ev for ev in evs if ev[0] != e or e != "pe"])

    def wait_all_on(self, eng, events):
        waits = []
        wd = self.waited[eng]
        for k, v in events:
            if wd.get(k, 0) < v:
                wd[k] = v
                waits.append((k, v))
        if waits:
            self.prog[eng].append((waits, None, None))

    def emit(self, stack):
        nc = self.nc
        for k in self.semkeys:
            self.sems[k] = stack.enter_context(nc.semaphore("s_" + k))
        block = stack.enter_context(nc.Block())
        sems = self.sems

        def run(engname):
            def body(h):
                for waits, fn, inc in self.prog[engname]:
                    for k, v in waits:
                        h.wait_ge(sems[k], v)
                    if fn is not None:
                        ins = fn(h)
                        ins.then_inc(sems[inc[0]], inc[1])
            return body
        block.sync(run("sp"))
        block.scalar(run("act"))
        block.vector(run("dve"))
        block.gpsimd(run("pool"))
        block.tensor(run("pe"))


class Arena:
    def __init__(self, ap):
        self.ap = ap
        self.n = ap.shape[1]
        self.off = 0

    def reset(self, off=0):
        self.off = off

    def f32(self, *shape, parts=128):
        n = int(np.prod(shape))
        a = self.ap[0:parts, self.off:self.off + n]
        self.off += n
        assert self.off <= self.n, ("arena overflow", self.off, self.n)
        self.mx = max(getattr(self, "mx", 0), self.off)
        if len(shape) == 2:
            a = a.rearrange("p (a b) -> p a b", a=shape[0])
        elif len(shape) == 3:
            a = a.rearrange("p (a b c) -> p a b c", a=shape[0], b=shape[1])
        elif len(shape) == 4:
            a = a.rearrange("p (a b c d) -> p a b c d", a=shape[0], b=shape[1], c=shape[2])
        return a

    def bf16(self, *shape, parts=128):
        n = int(np.prod(shape))
        nf = (n + 1) // 2
        a = self.ap[0:parts, self.off:self.off + nf].bitcast(BF16)[:, 0:n]
        self.off += nf
        assert self.off <= self.n, ("arena overflow", self.off, self.n)
        self.mx = max(getattr(self, "mx", 0), self.off)
        if len(shape) == 2:
            a = a.rearrange("p (a b) -> p a b", a=shape[0])
        elif len(shape) == 3:
            a = a.rearrange("p (a b c) -> p a b c", a=shape[0], b=shape[1])
        elif len(shape) == 4:
            a = a.rearrange("p (a b c d) -> p a b c d", a=shape[0], b=shape[1], c=shape[2])
        return a


L1F = 'psevdno'


def build_nc(stop_after=None):
    nc = bass.Bass("TRN2", target_bir_lowering=False)

    def din(name, shape, dt=F32):
        return nc.dram_tensor(name, list(shape), dt, kind="ExternalInput").ap()

    xw = din("xw", [WIN, D]); xo = din("xo", [NOB * 128, D]); xc = din("xc", [256, D])
    cvT = din("cvT", [128, 8, 2]); mo = din("mo", [128, NOB, 2])
    w_mod = din("w_mod", [2, D, 6 * D]); b_mod = din("b_mod", [2, 6 * D])
    ln_g = din("ln_g", [2, 2, D]); ln_b = din("ln_b", [2, 2, D])
    mlp_w1 = din("mlp_w1", [2, D, 4 * D]); mlp_w2 = din("mlp_w2", [2, 4 * D, D])
    w_in = din("w_in", [D, 3104]); theta = din("theta", [128, 2, 256]); wg = din("wg", [33, 512]); gng = din("gng", [128, 1])
    w_out0 = din("w_out0", [D, D]); w_qkv = din("w_qkv", [D, 1536]); sinkc = din("sinkc", [128, 8]); w_out1 = din("w_out1", [D, D])
    ident_d = din("ident", [128, 128]); LC_d = din("LC", [128, 2, 130]); MR_d = din("MR", [128, 2, 128])
    MRF_d = din("MRF", [128, 3, 128]); MK_d = din("MK4", [128, 4, 128]); MW_d = din("MW", [128, 2, 128]); perm_d = din("perm", [128, 128])
    ropeP = din("ropeP", [128, 100]); ropeI = din("ropeI", [128, 4])
    out = nc.dram_tensor("out", [WIN, D], F32, kind="ExternalOutput").ap()
    modscr = nc.dram_tensor("modscr", [2, 2, 6 * D], F32, kind="Internal").ap()
    sbscr = nc.dram_tensor("sbscr", [NSB, 128, 512], BF16, kind="Internal").ap()
    w1b = nc.dram_tensor("w1b", [2, D, 4 * D], BF16, kind="Internal").ap()
    w2b = nc.dram_tensor("w2b", [2, 4 * D, D], BF16, kind="Internal").ap()
    bw1b = [[Buf(f"w1b{l}_{i}") for i in range(8)] for l in range(2)]
    bw2b = [[Buf(f"w2b{l}_{i}") for i in range(32)] for l in range(2)]

    st = ExitStack()
    T = Tracker(nc)

    def sb(name, shape, dt=F32):
        return st.enter_context(nc.sbuf_tensor(name, list(shape), dt))

    xs = sb("xs", [128, NSB, D])
    bxs = [Buf(f"xs{i}") for i in range(NSB)]
    ident = sb("ident_s", [128, 128]); LC = sb("LC_s", [128, 2, 130]); MR = sb("MR_s", [128, 2, 128])
    MK4 = sb("MK4_s", [128, 4, 128]); MW = sb("MW_s", [128, 2, 128]); MRF = sb("MRF_s", [128, 3, 128]); onesf = sb("onesf", [1, 128])
    perm = sb("perm_s", [128, 128], BF16); onesb = sb("onesb", [128, 128], BF16)
    modT = sb("modT", [128, 2, 96]); gngs = sb("gngs", [128, 1]); sinkE = sb("sinkE", [128, 8])
    bconst = Buf("const"); bmodT = Buf("modT")
    NAR = 30280
    ar_t = sb("arena", [128, NAR])
    AR = Arena(ar_t[:, :])
    ps = [st.enter_context(nc.psum_tensor(f"ps{i}", [128, 512], F32)) for i in range(8)]
    bps = [Buf(f"ps{i}", psum=True) for i in range(8)]
    bmodscr = Buf("modscr"); bsbscr = [Buf(f"sbscr{i}") for i in range(NSB)]

    def mm(o, lhsT, rhs, start, stop, R, W, **kw):
        T.op("pe", lambda h: h.matmul(o, lhsT=lhsT, rhs=rhs, start=start, stop=stop, **kw), R, W)

    def act(o, i, func, R, W, **kw):
        T.op("act", lambda h: h.activation(out=o, in_=i, func=func, **kw), R, W)

    def tt(eng, o, a, b, op, R, W):
        T.op(eng, lambda h: h.tensor_tensor(out=o, in0=a, in1=b, op=op), R, W)

    def ts(eng, o, a, s1, s2, op0, op1, R, W):
        if s2 is None:
            T.op(eng, lambda h: h.tensor_scalar(out=o, in0=a, scalar1=s1, scalar2=None, op0=op0), R, W)
        else:
            T.op(eng, lambda h: h.tensor_scalar(out=o, in0=a, scalar1=s1, scalar2=s2, op0=op0, op1=op1), R, W)

    def stt(eng, o, a, s, b, op0, op1, R, W):
        T.op(eng, lambda h: h.scalar_tensor_tensor(out=o, in0=a, scalar=s, in1=b, op0=op0, op1=op1), R, W)

    def cp(eng, o, i, R, W):
        if eng == "act":
            T.op("act", lambda h: h.copy(out=o, in_=i), R, W)
        else:
            T.op(eng, lambda h: h.tensor_copy(out=o, in_=i), R, W)

    def dma(q, o, i, R, W):
        return T.dma(q, lambda h: h.dma_start(out=o, in_=i), R, W)

    def memset(eng, o, v, W):
        T.op(eng, lambda h: h.memset(o, v), (), W)

    dma("sp", ident[:], ident_d, [], [bconst]); dma("sp", LC[:], LC_d, [], [bconst]); dma("sp", MR[:], MR_d, [], [bconst])
    dma("sp", MK4[:], MK_d, [], [bconst]); dma("sp", MRF[:], MRF_d, [], [bconst]); memset("dve", onesf[:], 1.0, [bconst]); dma("sp", MW[:], MW_d, [], [bconst]); dma("pool", perm[:], perm_d, [], [bconst])
    dma("sp", gngs[:], gng, [], [bconst]); dma("sp", sinkE[:], sinkc, [], [bconst])
    memset("dve", onesb[:], 1.0, [bconst])
    act(sinkE[:], sinkE[:], AF.Exp, [bconst], [bconst])
    for i in range(NB):
        dma("sp", xs[:, i, :], xw[i * 128:(i + 1) * 128, :], [], [bxs[i]])
    for i in range(NCB):
        dma("sp", xs[:, NB + i, :], xc[i * 128:(i + 1) * 128, :], [], [bxs[NB + i]])

    AR.reset()
    win = AR.bf16(8, 3104); bwin = Buf("win")
    WIN_OFF = AR.off
    dma("pool", win, w_in.rearrange("(k p) c -> p k c", p=128), [], [bwin])

    AR.reset(WIN_OFF)
    cvs = AR.f32(8, 2); scT = AR.f32(8, 2); mrow = AR.f32(3, 512, parts=2); bmr = AR.f32(3, 512, parts=2)
    NWM = 3
    wm = [AR.f32(8, 512) for _ in range(NWM)]; modR = AR.f32(128, parts=96)
    bcvs = Buf("cvs"); bscT = Buf("scT"); bbm = [Buf(f"bm{i}") for i in range(NWM)]; bmrow = [Buf(f"mrow{i}") for i in range(NWM)]
    bwm = [Buf(f"wm{i}") for i in range(NWM)]; bmodR = Buf("modR")
    dma("sp", cvs, cvT, [], [bcvs])
    act(scT, cvs, AF.Silu, [bcvs], [bscT])
    gi = 0
    for l in range(2):
        wv = w_mod[l].rearrange("(k p) c -> p k c", p=128)
        for cg in range(12):
            wb = gi % NWM; gi += 1
            dma("sp" if gi % 2 == 0 else "pool", wm[wb], wv[:, :, cg * 512:(cg + 1) * 512], [], [bwm[wb]])
            dma("sp", bmr[:, wb, :], b_mod[l, cg * 512:(cg + 1) * 512].partition_broadcast(2), [], [bbm[wb]])
            pb = wb % 2
            for k in range(8):
                mm(ps[pb][0:2, :], scT[:, k, :], wm[wb][:, k, :], k == 0, k == 7, [bscT, bwm[wb]], [bps[pb]])
            tt("dve", mrow[:, wb, :], ps[pb][0:2, :], bmr[:, wb, :], ALU.add, [bps[pb], bbm[wb]], [bmrow[wb]])
            dma("act", modscr[l, :, cg * 512:(cg + 1) * 512], mrow[:, wb, :], [bmrow[wb]], [bmodscr])
        dma("sp", modR, modscr[l].rearrange("r (q p) -> (r q) p", p=128), [bmodscr], [bmodR])
        T.op("pe", lambda h: h.transpose(out=ps[2][:, 0:96], in_=modR, identity=ident[0:96, 0:96]), [bmodR, bconst], [bps[2]])
        cp("dve", modT[:, l, :], ps[2][:, 0:96], [bps[2]], [bmodT])
        for r in range(2):
            for v in (1, 4):
                o = r * 48 + v * 8
                ts("dve", modT[:, l, o:o + 8], modT[:, l, o:o + 8], 1.0, None, ALU.add, None, [bmodT], [bmodT])
    T.barrier()

    def precast_mlp(l):
        for i in range(8):
            dma("pool", w1b[l, i * 128:(i + 1) * 128, :], mlp_w1[l, i * 128:(i + 1) * 128, :], [], [bw1b[l][i]])
        for i in range(32):
            dma("pool", w2b[l, i * 128:(i + 1) * 128, :], mlp_w2[l, i * 128:(i + 1) * 128, :], [], [bw2b[l][i]])

    def make_hT(src, bsrc, dstf, bdst, l, r, which, banks=(6, 7)):
        sidx = r * 48 + (1 if which == 1 else 4) * 8
        bidx = r * 48 + (0 if which == 1 else 3) * 8
        for half in range(2):
            p = ps[banks[half]]; bp = bps[banks[half]]
            for j in range(4):
                k = half * 4 + j
                T.op("pe", lambda h, k=k, j=j, p=p: h.transpose(out=p[:, j * 128:(j + 1) * 128], in_=src[:, k * 128:(k + 1) * 128], identity=ident[:]), [bsrc, bconst], [bp])
            for j in range(4):
                k = half * 4 + j
                act(dstf(k), p[:, j * 128:(j + 1) * 128], AF.Identity, [bp, bmodT], [bdst],
                    scale=modT[:, l, sidx + k:sidx + k + 1], bias=modT[:, l, bidx + k:bidx + k + 1])

    def load_gates(l, sub, gt, bgt, cgt=None, bcgt=None):
        v = 2 if sub == 0 else 5
        dma("sp", gt[:, 0, :], modscr[l, 0, v * D:(v + 1) * D].partition_broadcast(128), [bmodscr], [bgt])
        dma("sp", gt[:, 1, :], ln_g[l, sub, :].partition_broadcast(128), [], [bgt])
        dma("sp", gt[:, 2, :], ln_b[l, sub, :].partition_broadcast(128), [], [bgt])
        if cgt is not None:
            dma("sp", cgt, modscr[l, 1, v * D:(v + 1) * D].partition_broadcast(128), [bmodscr], [bcgt])

    def layer_norm(blk, gt, bgt, st6, mv, bst):
        x = xs[:, blk, :]
        b = bxs[blk]
        T.op("dve", lambda h: h.bn_stats(out=st6[:, 0, :], in_=xs[:, blk, 0:512]), [b], [bst])
        T.op("dve", lambda h: h.bn_stats(out=st6[:, 1, :], in_=xs[:, blk, 512:1024]), [b], [bst])
        T.op("dve", lambda h: h.bn_aggr(out=mv[:, 0:2], in_=st6), [bst], [bst])
        act(mv[:, 2:3], mv[:, 1:2], AF.Sqrt, [bst], [bst], bias=LN_EPS, scale=1.0)
        T.op("dve", lambda h: h.reciprocal(out=mv[:, 2:3], in_=mv[:, 2:3]), [bst], [bst])
        ts("dve", x, x, mv[:, 0:1], mv[:, 2:3], ALU.subtract, ALU.mult, [b, bst], [b])
        tt("dve", x, x, gt[:, 1, :], ALU.mult, [b, bgt], [b])
        tt("pool", x, x, gt[:, 2, :], ALU.add, [b, bgt], [b])

    def resid_ln(blk, pa, pb, bpa, bpb, gate, bgate, gt, bgt, tmp, btmp, st6, mv, bst):
        tt("dve", tmp[:, 0:512], pa, gate[:, 0:512], ALU.mult, [bpa, bgate], [btmp])
        tt("dve", tmp[:, 512:1024], pb, gate[:, 512:1024], ALU.mult, [bpb, bgate], [btmp])
        stt("dve", xs[:, blk, :], xs[:, blk, :], ALPHA, tmp, ALU.mult, ALU.add, [bxs[blk], btmp], [bxs[blk]])
        layer_norm(blk, gt, bgt, st6, mv, bst)

    def mlp_phase(l, nblk):
        AR.reset()
        ntok = nblk * 128
        h2T = AR.bf16(8, NSB * 128); bh2T = [Buf(f"h2T{i}") for i in range(nblk)]
        W1g = [AR.bf16(8, 512), AR.bf16(8, 512)]; W2g = [AR.bf16(4, 1024), AR.bf16(4, 1024)]
        bW1 = [Buf("W1a"), Buf("W1b")]; bW2 = [Buf("W2a"), Buf("W2b")]
        aT = [AR.bf16(4, 512), AR.bf16(4, 512)]; baT = [Buf("aTa"), Buf("aTb")]
        rl = [AR.f32(512), AR.f32(512)]; brl = [Buf("rla"), Buf("rlb")]
        gt = AR.f32(3, D); bgt = Buf("gt"); cgt = AR.f32(D); bcgt = Buf("cgt")
        tmpg = [AR.f32(512), AR.f32(512)]; btmpg = [Buf("tga"), Buf("tgb")]
        st6 = AR.f32(2, 6); mv = AR.f32(4); bst = Buf("st")
        load_gates(l, 1, gt, bgt, cgt if nblk > NB else None, bcgt)
        for blk in range(nblk):
            r = 1 if blk >= NB else 0
            make_hT(xs[:, blk, :], bxs[blk], lambda k, blk=blk: h2T[:, k, blk * 128:(blk + 1) * 128], bh2T[blk], l, r, 2)
        for blk in range(nblk):
            T.op("dve", lambda h, blk=blk: h.tensor_scalar(out=xs[:, blk, :], in0=xs[:, blk, :], scalar1=ALPHA, scalar2=None, op0=ALU.mult), [bxs[blk]], [bxs[blk]])
        tiles = []
        t0 = 0
        while t0 < ntok:
            tn = min(512, ntok - t0)
            tiles.append((t0, tn)); t0 += tn
        w1v = w1b[l].rearrange("(k p) f -> p k f", p=128)
        w2v = w2b[l].rearrange("(c p) d -> p c d", p=128)
        cnt = 0

        def load_w(g):
            wb_ = g % 2
            dma("sp", W1g[wb_], w1v[:, :, g * 512:(g + 1) * 512], bw1b[l], [bW1[wb_]])
            dma("sp", W2g[wb_], w2v[:, g * 4:(g + 1) * 4, :], bw2b[l][g * 4:(g + 1) * 4], [bW2[wb_]])
        load_w(0)
        for ffg in range(8):
            wb = ffg % 2
            if ffg + 1 < 8:
                load_w(ffg + 1)
            for ti, (t0, tn) in enumerate(tiles):
                ab = ti % 2
                hb = [bh2T[b] for b in range(t0 // 128, (t0 + tn) // 128)]
                for fc in range(4):
                    pa = ps[fc % 2]; bpa = bps[fc % 2]
                    for k in range(8):
                        mm(pa[:, 0:tn], W1g[wb][:, k, fc * 128:(fc + 1) * 128], h2T[:, k, t0:t0 + tn], k == 0, k == 7, [bW1[wb]] + hb, [bpa])
                    rb = fc % 2
                    act(rl[rb][:, 0:tn], pa[:, 0:tn], AF.Relu, [bpa], [brl[rb]])
                    tt("dve", aT[ab][:, fc, 0:tn], rl[rb][:, 0:tn], rl[rb][:, 0:tn], ALU.mult, [brl[rb]], [baT[ab]])
                for tb in range(tn // 128):
                    blk = t0 // 128 + tb
                    gate = cgt if blk >= NB else gt[:, 0, :]
                    bg = bcgt if blk >= NB else bgt
                    for dh in range(2):
                        pi = 2 + (cnt % 4); cnt += 1
                        po = ps[pi]; bpo = bps[pi]
                        for fc in range(4):
                            mm(po[:, :], aT[ab][:, fc, tb * 128:(tb + 1) * 128], W2g[wb][:, fc, dh * 512:(dh + 1) * 512], fc == 0, fc == 3, [baT[ab], bW2[wb]], [bpo])
                        tg = cnt % 2
                        tt("dve", tmpg[tg], po[:, :], gate[:, dh * 512:(dh + 1) * 512], ALU.mult, [bpo, bg], [btmpg[tg]])
                        tt("pool", xs[:, blk, dh * 512:(dh + 1) * 512], xs[:, blk, dh * 512:(dh + 1) * 512], tmpg[tg], ALU.add, [bxs[blk], btmpg[tg]], [bxs[blk]])
        for blk in range(nblk):
            layer_norm(blk, gt, bgt, st6, mv, bst)
        T.barrier()

    def layer0_mixer():
        AR.reset(WIN_OFF)
        hT = AR.bf16(8, 128); bhT = Buf("hT")
        ktok = AR.bf16(512); bktok = Buf("ktok")
        V = AR.bf16(1024); bV = Buf("V")
        Kend = AR.bf16(2, 512); bKend = Buf("Kend")
        sp = AR.f32(2, 512); bsp = Buf("sp")
        scrA = AR.f32(2, 512); bscrA = Buf("scrA")
        scrB = AR.f32(2, 512); bscrB = Buf("scrB")
        lrT = AR.bf16(128, parts=33); blrT = Buf("lrT")
        Wg = AR.bf16(512, parts=33); bWg = Buf("Wg")
        dec = AR.f32(2, 4, 2); bdec = Buf("dec")
        Sf = [AR.f32(4, 128), AR.f32(4, 128)]; bSf = [Buf("Sf0"), Buf("Sf1")]
        off_shared = AR.off
        Sbacc = AR.f32(4, 128); bSbacc = Buf("Sbacc")
        ecf = AR.f32(2, 4); becf = Buf("ecf")
        xin = [AR.f32(D), AR.f32(D)]; bxin = [Buf("xin0"), Buf("xin1")]
        mos = AR.f32(NOB, 2); bmos = Buf("mos")
        thr = scrB[:, :, 0:256]; bthr = bscrB
        dma("pool", Wg, wg, [], [bWg])
        precast_mlp(0)
        precast_mlp(1)
        dma("sp", mos, mo, [], [bmos])
        dma("sp", thr, theta, [], [bthr])
        act(thr, thr, AF.Exp, [bthr], [bthr])
        act(thr, thr, AF.Ln, [bthr], [bthr], scale=-1.0, bias=1.0)
        for d in range(2):
            ts("dve", sp[:, d, 0:256], thr[:, d, :], -16.0, None, ALU.mult, None, [bthr], [bsp])
        memset("dve", lrT[32:33, :], 1.0, [blrT])
        for t_, b_ in ((Sf[0], bSf[0]), (Sf[1], bSf[1]), (Sbacc, bSbacc)):
            memset("dve", t_, 0.0, [b_])

        def kv_front(src, bsrc, r, mask_blk):
            make_hT(src, bsrc, lambda k: hT[:, k, :], bhT, 0, r, 1, banks=(0, 1))
            for k in range(8):
                mm(ps[0][:, :], hT[:, k, :], win[:, k, 512:1024], k == 0, k == 7, [bhT, bwin], [bps[0]])
            for k in range(8):
                mm(ps[1][:, :], hT[:, k, :], win[:, k, 1024:1536], k == 0, k == 7, [bhT, bwin], [bps[1]])
            for k in range(8):
                mm(ps[2][:, :], hT[:, k, :], win[:, k, 1536:2048], k == 0, k == 7, [bhT, bwin], [bps[2]])
            for k in range(8):
                mm(ps[3][0:32, 0:128], win[:, k, 3072:3104], hT[:, k, :], k == 0, k == 7, [bhT, bwin], [bps[3]])
            cp("act", ktok, ps[0][:, :], [bps[0]], [bktok])
            cp("dve", V[:, 0:512], ps[1][:, :], [bps[1]], [bV])
            cp("dve", V[:, 512:1024], ps[2][:, :], [bps[2]], [bV])
            cp("act", lrT[0:32, :], ps[3][0:32, 0:128], [bps[3]], [blrT])
            mm(ps[3][:, :], lrT[0:33, :], Wg[0:33, :], True, True, [blrT, bWg], [bps[3]])
            act(scrB.rearrange("p a b -> p (a b)")[:, 0:512], ps[3][:, :], AF.Exp, [bps[3]], [bscrB], scale=-1.0)
            act(sp[:, :, 256:512], scrB.rearrange("p a b -> p (a b)")[:, 0:512].rearrange("p (a b) -> p a b", a=2), AF.Ln, [bscrB], [bsp], bias=1.0, scale=1.0)
            if mask_blk is None:
                spm = sp; bspm = bsp
            else:
                spm = scrA; bspm = bscrA
                for d in range(2):
                    ts("dve", scrA[:, d, :], sp[:, d, :], mos[:, mask_blk, d:d + 1], None, ALU.mult, None, [bsp, bmos], [bscrA])
            for d in range(2):
                mm(ps[4 + d][:, :], MR[:, d, :], spm[:, d, :], True, True, [bconst, bspm], [bps[4 + d]])
            for d in range(2):
                act(scrB[:, d, :], ps[4 + d][:, :], AF.Exp, [bps[4 + d]], [bscrB])
            for d in range(2):
                if mask_blk is None:
                    tt("dve", Kend[:, d, :], ktok, scrB[:, d, :], ALU.mult, [bktok, bscrB], [bKend])
                else:
                    stt("dve", Kend[:, d, :], ktok, mos[:, mask_blk, d:d + 1], scrB[:, d, :], ALU.mult, ALU.mult, [bktok, bscrB, bmos], [bKend])
            return spm, bspm

        def kv_mms():
            bank = {(0, 0): 2, (0, 1): 4, (1, 0): 0, (1, 1): 6}
            for d in range(2):
                for hp in range(4):
                    for c in range(2):
                        bi = bank[(d, c)] + hp // 2
                        mm(ps[bi][:, (hp % 2) * 256:(hp % 2) * 256 + 256], Kend[64 * c:64 * c + 64, d, hp * 128:(hp + 1) * 128],
                           V[64 * c:64 * c + 64, hp * 256:(hp + 1) * 256], True, True, [bKend, bV], [bps[bi]])
            return bank

        def kv_ap(bank, d, c, hp, half):
            bi = bank[(d, c)] + hp // 2
            col = (hp % 2) * 256 + half * 128
            return ps[bi][64 * half:64 * half + 64, col:col + 128], bps[bi]

        def btot_dec(spm, bspm, pbank):
            for d in range(2):
                for hp in range(4):
                    o = (d * 4 + hp) * 2
                    mm(ps[pbank][:, o:o + 2], spm[:, d, hp * 128:(hp + 1) * 128], LC[:, 0, 128:130], True, True, [bspm, bconst], [bps[pbank]])
            act(dec.rearrange("p a b c -> p (a b c)"), ps[pbank][:, 0:16], AF.Exp, [bps[pbank]], [bdec])

        kvs_off = AR.off
        KVs = AR.bf16(NSB, 4, 128); bKVs = [Buf(f"KVs{i}") for i in range(NSB)]
        Dst = AR.f32(NSB, 4); bDst = Buf("Dst")
        sstage = [AR.bf16(4, 128), AR.bf16(4, 128)]; bsstage = [Buf("sst0"), Buf("sst1")]
        Srun = [AR.f32(4, 128), AR.f32(4, 128)]; bSrun = [Buf("Srun0"), Buf("Srun1")]
        Sctxb = AR.f32(4, 128); bSctxb = Buf("Sctxb")
        cur = [0]

        def passA_block(src, bsrc, r, mask_blk, fwd_state, bwd_mode, store_idx):
            spm, bspm = kv_front(src, bsrc, r, mask_blk)
            btot_dec(spm, bspm, 3)
            bank = kv_mms()
            if fwd_state:
                for c in range(2):
                    a = cur[0]; b = 1 - a
                    for hp in range(4):
                        for half in range(2):
                            rows = slice(64 * half, 64 * half + 64)
                            kv, bkv = kv_ap(bank, 0, c, hp, half)
                            stt("dve", Sf[b][rows, hp, :], Sf[a][rows, hp, :], dec[rows, 0, hp, c:c + 1], kv, ALU.mult, ALU.add, [bSf[a], bdec, bkv], [bSf[b]])
                    cur[0] = b
            if bwd_mode == "ptrick":
                raise AssertionError("unused")
            else:
                i = store_idx
                for hp in range(4):
                    for half in range(2):
                        rows = slice(64 * half, 64 * half + 64)
                        kv0, bkv0 = kv_ap(bank, 1, 0, hp, half)
                        kv1, bkv1 = kv_ap(bank, 1, 1, hp, half)
                        cp("act", KVs[rows, i, hp, :], kv0, [bkv0], [bKVs[i]])
                        stt("dve", KVs[rows, i, hp, :], kv1, dec[rows, 1, hp, 0:1], KVs[rows, i, hp, :], ALU.mult, ALU.add, [bkv1, bdec, bKVs[i]], [bKVs[i]])
                tt("dve", Dst[:, i, :], dec[:, 1, :, 0], dec[:, 1, :, 1], ALU.mult, [bdec], [bDst])

        def reverse_scan(S0, bS0, blocks):
            cpidx = 0
            cp("dve", Srun[0], S0, [bS0], [bSrun[0]])
            a = 0
            for n, i in enumerate(blocks):
                sbuf_ = sstage[n % 2]; bsb = bsstage[n % 2]
                cp("act", sbuf_, Srun[a], [bSrun[a]], [bsb])
                dma("sp", sbscr[i].rearrange("p (h e) -> p h e", h=4), sbuf_, [bsb], [bsbscr[i]])
                b = 1 - a
                for hp in range(4):
                    stt("dve", Srun[b][:, hp, :], Srun[a][:, hp, :], Dst[:, i, hp:hp + 1], KVs[:, i, hp, :], ALU.mult, ALU.add, [bSrun[a], bDst, bKVs[i]], [bSrun[b]])
                a = b
            return a

        for cb in range(NCB):
            passA_block(xs[:, NB + cb, :], bxs[NB + cb], 1, None, True, "store", NB + cb)
        memset("dve", Sctxb, 0.0, [bSctxb])
        a = reverse_scan(Sctxb, bSctxb, [NB + 1, NB])
        cp("dve", Sctxb, Srun[a], [bSrun[a]], [bSctxb])
        T.barrier()
        sub = Arena(ar_t[:, kvs_off:kvs_off + 5120])
        hT2 = [hT, sub.bf16(8, 128)]; bhT2 = [bhT, Buf("hT2")]
        ktok2 = [ktok, sub.bf16(512)]; bktok2 = [bktok, Buf("ktok2")]
        V2 = [V, sub.bf16(1024)]; bV2 = [bV, Buf("V2")]
        Kend2 = [Kend, sub.bf16(2, 512)]; bKend2 = [bKend, Buf("Kend2")]
        spm2 = [scrA, sub.f32(2, 512)]; bspm2 = [bscrA, Buf("spm2")]
        eb2 = [scrB, sub.f32(2, 512)]; beb2 = [bscrB, Buf("eb2")]
        crow = sub.f32(2, 512, parts=1); bcrow = Buf("crow")
        memset("dve", crow, 0.0, [bcrow])
        for bi in range(4, 8):
            memset("dve", ps[bi][:, :], 0.0, [bps[bi]])
        def o_front(ob):
            p = ob % 2
            hTp, bhTp = hT2[p], bhT2[p]
            dma("sp", xin[p], xo[ob * 128:(ob + 1) * 128, :], [], [bxin[p]])
            make_hT(xin[p], bxin[p], lambda k, hTp=hTp: hTp[:, k, :], bhTp, 0, 0, 1, banks=(0, 1))
            for k in range(8):
                mm(ps[2][:, :], hTp[:, k, :], win[:, k, 512:1024], k == 0, k == 7, [bhTp, bwin], [bps[2]])
            for k in range(8):
                mm(ps[3][:, :], hTp[:, k, :], win[:, k, 1024:1536], k == 0, k == 7, [bhTp, bwin], [bps[3]])
            for k in range(8):
                mm(ps[0][:, :], hTp[:, k, :], win[:, k, 1536:2048], k == 0, k == 7, [bhTp, bwin], [bps[0]])
            for k in range(8):
                mm(ps[1][0:32, 0:128], win[:, k, 3072:3104], hTp[:, k, :], k == 0, k == 7, [bhTp, bwin], [bps[1]])
            cp("act", ktok2[p], ps[2][:, :], [bps[2]], [bktok2[p]])
            cp("dve", V2[p][:, 0:512], ps[3][:, :], [bps[3]], [bV2[p]])
            cp("dve", V2[p][:, 512:1024], ps[0][:, :], [bps[0]], [bV2[p]])
            cp("act", lrT[0:32, :], ps[1][0:32, 0:128], [bps[1]], [blrT])
            mm(ps[1][:, :], lrT[0:33, :], Wg[0:33, :], True, True, [blrT, bWg], [bps[1]])
            ebf = eb2[p].rearrange("p a b -> p (a b)")
            act(ebf[:, 0:512], ps[1][:, :], AF.Exp, [bps[1]], [beb2[p]], scale=-1.0)
            act(sp[:, :, 256:512], ebf[:, 0:512].rearrange("p (a b) -> p a b", a=2), AF.Ln, [beb2[p]], [bsp], bias=1.0, scale=1.0)
            for d in range(2):
                ts("dve", spm2[p][:, d, :], sp[:, d, :], mos[:, ob, d:d + 1], None, ALU.mult, None, [bsp, bmos], [bspm2[p]])

        def o_gates(ob):
            p = ob % 2
            for d in range(2):
                mm(ps[2 + d][:, :], MRF[:, d, :], spm2[p][:, d, :], True, False, [bconst, bspm2[p]], [bps[2 + d]])
                mm(ps[2 + d][:, :], onesf[0:1, :], crow[0:1, d, :], False, True, [bconst, bcrow], [bps[2 + d]])
            for d in range(2):
                mm(ps[0][32 * d:32 * d + 1, :], MRF[:, 2, 0:1], spm2[p][:, d, :], True, True, [bconst, bspm2[p]], [bps[0]])
            for d in range(2):
                act(eb2[p][:, d, :], ps[2 + d][:, :], AF.Exp, [bps[2 + d]], [beb2[p]])
            for d in range(2):
                stt("dve", Kend2[p][:, d, :], ktok2[p], mos[:, ob, d:d + 1], eb2[p][:, d, :], ALU.mult, ALU.mult, [bktok2[p], beb2[p], bmos], [bKend2[p]])
            for d in range(2):
                tt("dve", crow[0:1, d, :], crow[0:1, d, :], ps[0][32 * d:32 * d + 1, :], ALU.add, [bcrow, bps[0]], [bcrow])

        def o_kv(ob):
            p = ob % 2
            for d in range(2):
                for hp in range(4):
                    bi = 4 + 2 * d + hp // 2
                    mm(ps[bi][:, (hp % 2) * 256:(hp % 2) * 256 + 256], Kend2[p][:, d, hp * 128:(hp + 1) * 128], V2[p][:, hp * 256:(hp + 1) * 256],
                       False, False, [bKend2[p], bV2[p]], [bps[bi]], skip_group_check=True)

        o_front(0)
        for ob in range(NOB):
            o_gates(ob)
            if ob + 1 < NOB:
                o_front(ob + 1)
            o_kv(ob)
        for d in range(2):
            for hp in range(4):
                mm(ps[0][:, d * 4 + hp:d * 4 + hp + 1], crow[0:1, d, hp * 128:(hp + 1) * 128], onesf[0:1, 0:1], True, True, [bcrow, bconst], [bps[0]])
        act(ecf.rearrange("p a b -> p (a b)"), ps[0][:, 0:8], AF.Exp, [bps[0]], [becf])
        a = cur[0]; b = 1 - a
        for hp in range(4):
            for half in range(2):
                rows = slice(64 * half, 64 * half + 64)
                col = (hp % 2) * 256 + half * 128
                stt("dve", Sf[b][rows, hp, :], Sf[a][rows, hp, :], ecf[rows, 0, hp:hp + 1], ps[4 + hp // 2][rows, col:col + 128], ALU.mult, ALU.add,
                    [bSf[a], becf, bps[4 + hp // 2]], [bSf[b]])
                stt("dve", Sbacc[rows, hp, :], Sctxb[rows, hp, :], ecf[rows, 1, hp:hp + 1], ps[6 + hp // 2][rows, col:col + 128], ALU.mult, ALU.add,
                    [bSctxb, becf, bps[6 + hp // 2]], [bSbacc])
        cur[0] = b
        T.barrier()
        for blk in range(NB):
            passA_block(xs[:, blk, :], bxs[blk], 0, None, False, "store", blk)
        reverse_scan(Sbacc, bSbacc, list(range(NB - 1, -1, -1)))
        T.barrier()

        AR.reset(off_shared)
        wout = AR.bf16(8, D); bwout = Buf("wout")
        dma("pool", wout, w_out0.rearrange("(k p) c -> p k c", p=128), [], [bwout])
        gt = AR.f32(3, D); bgt = Buf("gt")
        load_gates(0, 0, gt, bgt)
        qkT = AR.bf16(2, 4, 128); bqkT = Buf("qkT")
        sg = AR.bf16(8, 128); bsg = Buf("sg")
        QK = AR.bf16(2, 2, 4, 128); bQK = Buf("QK")
        PT = AR.bf16(2, 512); bPT = [Buf("PTa"), Buf("PTb")]
        Sbf = AR.bf16(2, 2, 4, 128); bSbf = Buf("Sbf")
        osq = PT
        yT = AR.bf16(8, 128); byT = Buf("yT")
        st6 = AR.f32(2, 6); mv = AR.f32(4); bst = Buf("st")

        def passB_block(blk, r, gate, bgate):
            src = xs[:, blk, :]; bsrc = bxs[blk]
            dma("sp", Sbf[:, 1, 0, :, :], sbscr[blk].rearrange("p (h e) -> p h e", h=4), [bsbscr[blk]], [bSbf])
            spm, bspm = kv_front(src, bsrc, r, None)
            qcols = [0, 128, 256, 384]; kcols = [512, 640, 768, 896]
            for hp in range(4):
                for k in range(8):
                    mm(ps[0][:, hp * 128:(hp + 1) * 128], win[:, k, qcols[hp]:qcols[hp] + 128], hT[:, k, :], k == 0, k == 7, [bwin, bhT], [bps[0]])
            for hp in range(4):
                for k in range(8):
                    mm(ps[1][:, hp * 128:(hp + 1) * 128], win[:, k, kcols[hp]:kcols[hp] + 128], hT[:, k, :], k == 0, k == 7, [bwin, bhT], [bps[1]])
            for i in range(4):
                for k in range(8):
                    mm(ps[2][:, i * 128:(i + 1) * 128], win[:, k, 2048 + i * 128:2048 + (i + 1) * 128], hT[:, k, :], k == 0, k == 7, [bwin, bhT], [bps[2]])
            for i in range(4):
                for k in range(8):
                    mm(ps[3][:, i * 128:(i + 1) * 128], win[:, k, 2560 + i * 128:2560 + (i + 1) * 128], hT[:, k, :], k == 0, k == 7, [bwin, bhT], [bps[3]])
            T.op("act", lambda h: h.mul(out=qkT[:, 0, :, :].rearrange("p a b -> p (a b)"), in_=ps[0][:, :], mul=0.125), [bps[0]], [bqkT])
            cp("dve", qkT[:, 1, :, :].rearrange("p a b -> p (a b)"), ps[1][:, :], [bps[1]], [bqkT])
            act(sg[:, 0:4, :].rearrange("p a b -> p (a b)"), ps[2][:, :], AF.Silu, [bps[2]], [bsg])
            act(sg[:, 4:8, :].rearrange("p a b -> p (a b)"), ps[3][:, :], AF.Silu, [bps[3]], [bsg])
            for d in range(2):
                for hp in range(4):
                    mm(ps[d][:, hp * 128:(hp + 1) * 128], sp[:, d, hp * 128:(hp + 1) * 128], LC[:, d, 0:128], True, True, [bsp, bconst], [bps[d]])
            btot_dec(sp, bsp, 2)
            for d in range(2):
                act(scrA.rearrange("p a b -> p (a b)")[:, 0:512], ps[d][:, :], AF.Exp, [bps[d]], [bscrA])
                tt("dve", QK[:, d, 0, :, :].rearrange("p a b -> p (a b)"), qkT[:, 0, :, :].rearrange("p a b -> p (a b)"), scrA.rearrange("p a b -> p (a b)")[:, 0:512], ALU.mult, [bqkT, bscrA], [bQK])
                act(scrA.rearrange("p a b -> p (a b)")[:, 512:1024], ps[d][:, :], AF.Exp, [bps[d]], [bscrA], scale=-1.0)
                tt("dve", QK[:, d, 1, :, :].rearrange("p a b -> p (a b)"), qkT[:, 1, :, :].rearrange("p a b -> p (a b)"), scrA.rearrange("p a b -> p (a b)")[:, 512:1024], ALU.mult, [bqkT, bscrA], [bQK])
            bank = kv_mms()
            a = cur[0]; b = 1 - a
            cp("act", Sbf[:, 0, 0, :, :], Sf[a], [bSf[a]], [bSbf])
            for hp in range(4):
                for half in range(2):
                    rows = slice(64 * half, 64 * half + 64)
                    kv, bkv = kv_ap(bank, 0, 0, hp, half)
                    stt("dve", Sf[b][rows, hp, :], Sf[a][rows, hp, :], dec[rows, 0, hp, 0:1], kv, ALU.mult, ALU.add, [bSf[a], bdec, bkv], [bSf[b]])
            cp("act", Sbf[:, 0, 1, :, :], Sf[b], [bSf[b]], [bSbf])
            for hp in range(4):
                for half in range(2):
                    rows = slice(64 * half, 64 * half + 64)
                    kv, bkv = kv_ap(bank, 0, 1, hp, half)
                    stt("dve", Sf[a][rows, hp, :], Sf[b][rows, hp, :], dec[rows, 0, hp, 1:2], kv, ALU.mult, ALU.add, [bSf[b], bdec, bkv], [bSf[a]])
            for hp in range(4):
                for half in range(2):
                    rows = slice(64 * half, 64 * half + 64)
                    kv, bkv = kv_ap(bank, 1, 1, hp, half)
                    stt("dve", Sbf[rows, 1, 1, hp, :], Sbf[rows, 1, 0, hp, :], dec[rows, 1, hp, 1:2], kv, ALU.mult, ALU.add, [bSbf, bdec, bkv], [bSbf])
            for hpp in range(2):
                for hl in range(2):
                    hp = hpp * 2 + hl
                    for d in range(2):
                        for X in range(2):
                            rows = slice(64 * X, 64 * X + 64)
                            col = (hl * 2 + d) * 128
                            mm(ps[X][:, col:col + 128], QK[rows, d, 1, hp, :], QK[rows, d, 0, hp, :], True, True, [bQK], [bps[X]])
                for X in range(2):
                    tt("dve", PT[:, X, :], ps[X][:, :], MK4.rearrange("p a b -> p (a b)"), ALU.mult, [bps[X], bconst], [bPT[X]])
                for hl in range(2):
                    hp = hpp * 2 + hl
                    for X in range(2):
                        rows = slice(64 * X, 64 * X + 64)
                        o = ps[2 + X][:, hp * 128:(hp + 1) * 128]; bo = bps[2 + X]
                        vv = V[:, hp * 256 + X * 128:hp * 256 + X * 128 + 128]
                        R = [bV, bPT[X], bSbf, bQK]
                        mm(o, vv, PT[:, X, (hl * 2 + 0) * 128:(hl * 2 + 1) * 128], True, False, R, [bo])
                        mm(o, vv, PT[:, X, (hl * 2 + 1) * 128:(hl * 2 + 2) * 128], False, False, R, [bo])
                        mm(o[:, 0:64], Sbf[rows, 0, 0, hp, :], QK[rows, 0, 0, hp, 0:64], False, False, R, [bo])
                        mm(o[:, 64:128], Sbf[rows, 0, 1, hp, :], QK[rows, 0, 0, hp, 64:128], False, False, R, [bo])
                        mm(o[:, 64:128], Sbf[rows, 1, 0, hp, :], QK[rows, 1, 0, hp, 64:128], False, False, R, [bo])
                        mm(o[:, 0:64], Sbf[rows, 1, 1, hp, :], QK[rows, 1, 0, hp, 0:64], False, True, R, [bo])
            for X in range(2):
                act(osq[:, X, :], ps[2 + X][:, :], AF.Square, [bps[2 + X]], [bPT[X]])
            for X in range(2):
                mm(ps[4 + X][:, :], onesb[:, :], osq[:, X, :], True, True, [bconst, bPT[X]], [bps[4 + X]])
            for X in range(2):
                act(scrA[:, X, :], ps[4 + X][:, :], AF.Sqrt, [bps[4 + X]], [bscrA], scale=1.0 / 128.0, bias=RMS_EPS)
            T.op("dve", lambda h: h.reciprocal(out=scrA.rearrange("p a b -> p (a b)"), in_=scrA.rearrange("p a b -> p (a b)")), [bscrA], [bscrA])
            for X in range(2):
                tt("dve", scrB[:, X, 0:256], ps[2 + X][:, 0:256], scrA[:, X, 0:256], ALU.mult, [bps[2 + X], bscrA], [bscrB])
                stt("dve", scrB[:, X, 256:512], ps[2 + X][:, 256:512], gngs[:, 0:1], scrA[:, X, 256:512], ALU.mult, ALU.mult, [bps[2 + X], bscrA, bconst], [bscrB])
                tt("dve", yT.rearrange("p (h x) t -> p h x t", x=2)[:, :, X, :], scrB[:, X, :].rearrange("p (h t) -> p h t", h=4),
                   sg.rearrange("p (h x) t -> p h x t", x=2)[:, :, X, :], ALU.mult, [bscrB, bsg], [byT])
            for dh in range(2):
                for fc in range(8):
                    mm(ps[6 + dh][:, :], yT[:, fc, :], wout[:, fc, dh * 512:(dh + 1) * 512], fc == 0, fc == 7, [byT, bwout], [bps[6 + dh]])
            resid_ln(blk, ps[6][:, :], ps[7][:, :], bps[6], bps[7], gate, bgate, gt, bgt,
                     scrA.rearrange("p a b -> p (a b)"), bscrA, st6, mv, bst)

        for blk in range(NB):
            passB_block(blk, 0, gt[:, 0, :], bgt)
        dma("sp", gt[:, 0, :], modscr[0, 1, 2 * D:3 * D].partition_broadcast(128), [bmodscr], [bgt])
        memset("dve", Sf[cur[0]], 0.0, [bSf[cur[0]]])
        for cb in range(NCB):
            passB_block(NB + cb, 1, gt[:, 0, :], bgt)
        T.barrier()

    def layer1_mixer():
        AR.reset()
        wq = AR.bf16(8, 1536); bwq = Buf("wq")
        dma("pool", wq, w_qkv.rearrange("(k p) c -> p k c", p=128), [], [bwq])
        wout = AR.bf16(8, D); bwout = Buf("wout")
        dma("pool", wout, w_out1.rearrange("(k p) c -> p k c", p=128), [], [bwout])
        gt = AR.f32(3, D); bgt = Buf("gt")
        load_gates(1, 0, gt, bgt)
        kT = AR.bf16(2, WIN); bkT = [Buf(f"kT{i}") for i in range(NB)]
        Vw = AR.bf16(NB, 256); bVw = [Buf(f"Vw{i}") for i in range(NB)]
        kTc = AR.bf16(2, 256); Vc = AR.bf16(2, 256); bctx = Buf("ctxkv")
        hT = [AR.bf16(8, 128), AR.bf16(8, 128)]; bhT = [Buf("hTa"), Buf("hTb")]
        qT = [AR.bf16(8, 128), AR.bf16(8, 128)]; bqT = [Buf("qTa"), Buf("qTb")]
        tmp = AR.f32(D); btmp = Buf("tmp")
        cosT = AR.f32(WIN); sinT = AR.f32(WIN); ridx = AR.f32(4); invf = AR.f32(1); brope = Buf("rope")
        cc = AR.f32(100); sc_ = AR.f32(100); tc_ = AR.f32(100); bcmp = Buf("ropec")
        dma("sp", cc, ropeP, [], [bcmp])
        dma("sp", ridx, ropeI, [], [bcmp])
        act(invf, ridx[:, 0:1], AF.Exp, [bcmp], [bcmp], scale=-float(np.log(10000.0)) / 16.0)
        ts("dve", cc, cc, invf[:, 0:1], None, ALU.mult, None, [bcmp], [bcmp])
        act(sc_, cc, AF.Sin, [bcmp], [bcmp], scale=1.0 / 128.0)
        act(cc, cc, AF.Sin, [bcmp], [bcmp], scale=1.0 / 256.0)
        tt("dve", cc, cc, cc, ALU.mult, [bcmp], [bcmp])
        ts("dve", cc, cc, -2.0, 1.0, ALU.mult, ALU.add, [bcmp], [bcmp])
        for _ in range(7):
            tt("dve", tc_, sc_, sc_, ALU.mult, [bcmp], [bcmp])
            stt("dve", sc_, sc_, 2.0, cc, ALU.mult, ALU.mult, [bcmp], [bcmp])
            ts("dve", cc, tc_, -2.0, 1.0, ALU.mult, ALU.add, [bcmp], [bcmp])
        ts("dve", sc_, sc_, ridx[:, 1:2], None, ALU.mult, None, [bcmp], [bcmp])
        for full, cmpt in ((cosT, cc), (sinT, sc_)):
            f3 = full.rearrange("p (r c) -> p r c", r=36)
            aR = cmpt[:, 0:36]; aC = cmpt[:, 36:100]
            bR = bass.AP(aR.tensor, aR.offset, [list(aR.ap[0]), [1, 36], [0, 64]])
            bC = bass.AP(aC.tensor, aC.offset, [list(aC.ap[0]), [0, 36], [1, 64]])
            ts("dve", f3, bR, ridx[:, 2:3], None, ALU.mult, None, [bcmp], [brope])
            stt("dve", f3, bC, ridx[:, 3:4], f3, ALU.mult, ALU.add, [bcmp, brope], [brope])

        def bc4(t, blk, nch):
            a = t[:, blk * 128:(blk + 1) * 128]
            return bass.AP(a.tensor, a.offset, [list(a.ap[0]), [0, nch], [1, 128]])
        tb16 = AR.bf16(512); btb = Buf("tb16")
        r1 = AR.f32(512); r2 = AR.f32(512); br1 = Buf("r1"); br2 = Buf("r2")
        E = AR.bf16(2, 640); bE = [Buf("Ea"), Buf("Eb")]
        rd = AR.f32(128); brd = Buf("rd")
        attT = AR.bf16(8, 128); battT = Buf("attT")
        st6 = AR.f32(2, 6); mv = AR.f32(4); bst = Buf("st")

        for cb in range(NCB):
            make_hT(xs[:, NB + cb, :], bxs[NB + cb], lambda k: hT[0][:, k, :], bhT[0], 1, 1, 1)
            for m in range(2):
                for k in range(8):
                    mm(ps[0][:, m * 128:(m + 1) * 128], wq[:, k, 1024 + m * 128:1024 + (m + 1) * 128], hT[0][:, k, :], k == 0, k == 7, [bwq, bhT[0]], [bps[0]])
            for m in range(2):
                cp("act", kTc[:, m, cb * 128:(cb + 1) * 128], ps[0][:, m * 128:(m + 1) * 128], [bps[0]], [bctx])
            for k in range(8):
                mm(ps[1][:, 0:256], hT[0][:, k, :], wq[:, k, 1280:1536], k == 0, k == 7, [bwq, bhT[0]], [bps[1]])
            cp("dve", Vc[:, cb, :], ps[1][:, 0:256], [bps[1]], [bctx])

        def rope(pbank, ncol, blk, outap, bout, nch):
            v3 = lambda t: t[:, 0:ncol].rearrange("p (a b) -> p a b", a=nch)
            cp("act", tb16[:, 0:ncol], ps[pbank][:, 0:ncol], [bps[pbank]], [btb])
            mm(ps[5][:, 0:ncol], perm[:, :], tb16[:, 0:ncol], True, True, [bconst, btb], [bps[5]])
            tt("dve", v3(r1), v3(ps[pbank]), bc4(cosT, blk, nch), ALU.mult, [bps[pbank], brope], [br1])
            tt("dve", v3(r2), v3(ps[5]), bc4(sinT, blk, nch), ALU.mult, [bps[5], brope], [br2])
            tt("dve", outap, v3(r1), v3(r2), ALU.add, [br1, br2], [bout])

        def proj_block(blk):
            hb = blk % 2
            make_hT(xs[:, blk, :], bxs[blk], lambda k: hT[hb][:, k, :], bhT[hb], 1, 0, 1)
            for m in range(2):
                for k in range(8):
                    mm(ps[0][:, m * 128:(m + 1) * 128], wq[:, k, 1024 + m * 128:1024 + (m + 1) * 128], hT[hb][:, k, :], k == 0, k == 7, [bwq, bhT[hb]], [bps[0]])
            for k in range(8):
                mm(ps[1][:, 0:256], hT[hb][:, k, :], wq[:, k, 1280:1536], k == 0, k == 7, [bwq, bhT[hb]], [bps[1]])
            cp("act", Vw[:, blk, :], ps[1][:, 0:256], [bps[1]], [bVw[blk]])
            rope(0, 256, blk, kT[:, :, blk * 128:(blk + 1) * 128], bkT[blk], 2)
            for half in range(2):
                for j in range(4):
                    c = half * 4 + j
                    for k in range(8):
                        mm(ps[2 + half][:, j * 128:(j + 1) * 128], wq[:, k, c * 128:(c + 1) * 128], hT[hb][:, k, :], k == 0, k == 7, [bwq, bhT[hb]], [bps[2 + half]])
                rope(2 + half, 512, blk, qT[hb][:, half * 4:half * 4 + 4, :], bqT[hb], 4)

        def attn_block(qb):
            hb = qb % 2
            loc = []
            if qb > 0:
                loc.append((qb - 1, 0))
            loc.append((qb, None))
            if qb < NB - 1:
                loc.append((qb + 1, 1))
            nl = len(loc)
            for j in range(8):
                m = j // 4
                for X in range(2 if 's' in L1F else 0):
                    rows = slice(64 * X, 64 * X + 64)
                    for li, (kb, _) in enumerate(loc):
                        mm(ps[X][:, li * 128:(li + 1) * 128], kT[rows, m, kb * 128:(kb + 1) * 128], qT[hb][rows, j, :], True, True, [bkT[kb], bqT[hb]], [bps[X]])
                for X in range(2 if 's' in L1F else 0):
                    rows = slice(64 * X, 64 * X + 64)
                    for cb in range(NCB):
                        mm(ps[2 + X][:, cb * 128:(cb + 1) * 128], kTc[rows, m, cb * 128:(cb + 1) * 128], qT[hb][rows, j, :], True, True, [bctx, bqT[hb]], [bps[2 + X]])
                for X in range(2 if 'e' in L1F else 0):
                    act(E[:, X, 0:nl * 128], ps[X][:, 0:nl * 128], AF.Exp, [bps[X]], [bE[X]], scale=0.125)
                    act(E[:, X, 384:640], ps[2 + X][:, 0:256], AF.Exp, [bps[2 + X]], [bE[X]], scale=0.125)
                    for li, (kb, mk) in enumerate(loc):
                        if mk is not None:
                            tt("dve", E[:, X, li * 128:(li + 1) * 128], E[:, X, li * 128:(li + 1) * 128], MW[:, mk, :], ALU.mult, [bE[X], bconst], [bE[X]])
                for X in range(2):
                    rows = slice(64 * X, 64 * X + 64)
                    g = 2 * m + X
                    n = nl + NCB
                    for li, (kb, _) in enumerate(loc if 'v' in L1F else []):
                        mm(ps[4][rows, 0:128], Vw[:, kb, g * 64:(g + 1) * 64], E[:, X, li * 128:(li + 1) * 128], li == 0, False, [bVw[kb], bE[X]], [bps[4]])
                    for cb in range(NCB if 'v' in L1F else 0):
                        mm(ps[4][rows, 0:128], Vc[:, cb, g * 64:(g + 1) * 64], E[:, X, 384 + cb * 128:384 + (cb + 1) * 128], False, cb == NCB - 1, [bctx, bE[X]], [bps[4]])
                    for li in range(nl if 'd' in L1F else 0):
                        mm(ps[6][rows, 0:128], onesb[:, 0:64], E[:, X, li * 128:(li + 1) * 128], li == 0, False, [bconst, bE[X]], [bps[6]])
                    for cb in range(NCB if 'd' in L1F else 0):
                        mm(ps[6][rows, 0:128], onesb[:, 0:64], E[:, X, 384 + cb * 128:384 + (cb + 1) * 128], False, cb == NCB - 1, [bconst, bE[X]], [bps[6]])
                if 'n' not in L1F:
                    continue
                ts("dve", rd, ps[6][:, 0:128], sinkE[:, j:j + 1], None, ALU.add, None, [bps[6], bconst], [brd])
                T.op("dve", lambda h: h.reciprocal(out=rd, in_=rd), [brd], [brd])
                tt("dve", attT[:, j, :], ps[4][:, 0:128], rd, ALU.mult, [bps[4], brd], [battT])
            if 'o' not in L1F:
                return
            for dh in range(2):
                for fc in range(8):
                    mm(ps[5 + 2 * dh][:, :], attT[:, fc, :], wout[:, fc, dh * 512:(dh + 1) * 512], fc == 0, fc == 7, [battT, bwout], [bps[5 + 2 * dh]])
            resid_ln(qb, ps[5][:, :], ps[7][:, :], bps[5], bps[7], gt[:, 0, :], bgt, gt, bgt, tmp, btmp, st6, mv, bst)

        for blk in range(NB + 1):
            if blk < NB:
                proj_block(blk)
            if blk >= 1:
                attn_block(blk - 1)
        T.barrier()

    if stop_after == "L1only":
        layer1_mixer()
    elif stop_after != "M":
        layer0_mixer()
        if stop_after != "L0A":
            mlp_phase(0, NSB)
            if stop_after != "L0":
                layer1_mixer()
                if stop_after != "L1A":
                    mlp_phase(1, NB)
    evs = []
    for blk in range(NB):
        evs.append(dma("sp", out[blk * 128:(blk + 1) * 128, :], xs[:, blk, :], [bxs[blk]], []))
    dbg = None
    if stop_after is not None:
        dbg = nc.dram_tensor("dbg", [128, 2 * 96 + 2 * D], F32, kind="ExternalOutput").ap()
        evs.append(dma("sp", dbg[:, 0:192], modT[:, :, :].rearrange("p a b -> p (a b)"), [bmodT], []))
        evs.append(dma("sp", dbg[:, 192:192 + D], xs[:, NB, :], [bxs[NB]], []))
        evs.append(dma("sp", dbg[:, 192 + D:192 + 2 * D], xs[:, NB + 1, :], [bxs[NB + 1]], []))
    T.wait_all_on("sp", evs)
    T.emit(st)
    st.close()
    return nc, T


def _consts():
    s = np.arange(128)[:, None]; c = np.arange(128)[None, :]
    same = (s // 64) == (c // 64)
    LC = np.zeros((128, 2, 130), np.float32)
    LC[:, 0, :128] = np.where(same & (s <= c), -1.0 / 16, 0.0)
    LC[:, 1, :128] = np.where(same & (s >= c), -1.0 / 16, 0.0)
    for d in range(2):
        LC[:64, d, 128] = -1.0 / 16
        LC[64:, d, 129] = -1.0 / 16
    MR = np.zeros((128, 2, 128), np.float32)
    MR[:, 0, :] = np.where(same & (s > c), -1.0 / 16, 0.0)
    MR[:, 1, :] = np.where(same & (s < c), -1.0 / 16, 0.0)
    Mf = np.where(same & (s <= c), 1.0, 0.0).astype(np.float32)
    Mb = np.where(same & (s > c), 1.0, 0.0).astype(np.float32)
    MK4 = np.stack([Mf, Mb, Mf, Mb], axis=1)
    MRF = np.zeros((128, 3, 128), np.float32)
    MRF[:, 0, :] = np.where(s > c, -1.0 / 16, 0.0)
    MRF[:, 1, :] = np.where(s < c, -1.0 / 16, 0.0)
    MRF[:, 2, :] = -1.0 / 16
    MW = np.stack([np.where(s >= c, 1.0, 0.0), np.where(s <= c, 1.0, 0.0)], axis=1).astype(np.float32)
    perm = np.zeros((128, 128), np.float32)
    for m in range(128):
        w = m % 32
        p = m + 16 if w < 16 else m - 16
        perm[p, m] = 1.0
    return dict(ident=np.eye(128, dtype=np.float32), LC=LC, MR=MR, MRF=MRF, MK4=np.ascontiguousarray(MK4), MW=np.ascontiguousarray(MW), perm=perm)


def _rope_tables(w0):
    r0 = w0 // 64
    P = np.zeros((128, 100), np.float32); I = np.zeros((128, 4), np.float32)
    for d in range(128):
        dd = d % 64
        sec = dd // 32; w = dd % 32
        if sec == 0:
            P[d, 0:36] = np.arange(r0, r0 + 36)
        else:
            P[d, 36:100] = np.arange(64)
        I[d, 0] = w % 16
        I[d, 1] = -1.0 if w < 16 else 1.0
        I[d, 2] = 1.0 if sec == 0 else 0.0
        I[d, 3] = 1.0 if sec == 1 else 0.0
    return P, I


_WIN_COLS = np.concatenate([np.arange(0, 256), np.arange(1536, 1792), np.arange(256, 512), np.arange(1792, 2048),
                            np.arange(512, 1024), np.arange(2048, 2560), np.arange(1024, 1536), np.arange(2560, 3072), np.arange(3072, 3104)])
_CACHE = {}


def _prep_inputs(inp):
    f = lambda a: np.ascontiguousarray(np.asarray(a, dtype=np.float32))
    x = f(inp["x"]); c = f(inp["c"]); ctx = f(inp["ctx"]); c_ctx = f(inp["c_ctx"])
    cst = _consts()
    order = []
    for j in range(8):
        m, i = j // 4, j % 4
        order += [(2 * m) * 4 + i, (2 * m + 1) * 4 + i]
    qcols = np.concatenate([np.arange(h * 64, (h + 1) * 64) for h in order])
    w_qkv = f(inp["od_w_qkv"])[0]
    w_qkv_p = np.ascontiguousarray(np.concatenate([w_qkv[:, qcols], w_qkv[:, 1024:]], axis=1))
    w_out1 = np.ascontiguousarray(f(inp["od_w_out"])[0][qcols, :])
    sink = f(inp["od_sink"])[0]
    sinkc = np.zeros((128, 8), np.float32)
    for j in range(8):
        sinkc[:64, j] = sink[order[2 * j]]; sinkc[64:, j] = sink[order[2 * j + 1]]
    th = f(inp["ev_ret_theta"])[0]
    theta = np.ascontiguousarray(np.broadcast_to(np.repeat(th, 64, axis=1)[None], (128, 2, 256))).astype(np.float32)
    gkw = f(inp["ev_gla_gk_w"])[0]; gkb = f(inp["ev_gla_gk_b"])[0]
    wg = np.zeros((33, 512), np.float32)
    wg[0:16, 0:256] = gkw[0]; wg[16:32, 256:512] = gkw[1]; wg[32, 0:256] = gkb[0]; wg[32, 256:512] = gkb[1]
    gng = f(inp["ev_gla_norm_g"])[0].reshape(128, 1)
    shared = dict(w_mod=f(inp["w_mod"]), b_mod=f(inp["b_mod"]), ln_g=f(inp["ln_g"]), ln_b=f(inp["ln_b"]),
                  mlp_w1=f(inp["mlp_w1"]), mlp_w2=f(inp["mlp_w2"]), w_in=np.ascontiguousarray(f(inp["ev_w_in"])[0][:, _WIN_COLS]), theta=theta, wg=wg, gng=np.ascontiguousarray(gng),
                  w_out0=f(inp["ev_w_out"])[0], w_qkv=w_qkv_p, sinkc=sinkc, w_out1=w_out1, **cst)
    maps = []
    offs = []
    for core in range(8):
        b, j = core // 4, core % 4
        w0 = min(max(j * 2048 - 128, 0), 8192 - WIN)
        offs.append(j * 2048 - w0)
        xw = x[b, w0:w0 + WIN]
        pre = x[b, :w0].reshape(-1, 128, D)[::-1].reshape(-1, D)
        xo = np.concatenate([pre, x[b, w0 + WIN:]], axis=0)
        mo = np.zeros((128, NOB, 2), np.float32)
        npre = w0 // 128
        mo[:, :npre, 0] = 1.0; mo[:, npre:, 1] = 1.0
        cv = np.stack([c[b], c_ctx], axis=0)
        cvT = np.ascontiguousarray(cv.reshape(2, 8, 128).transpose(2, 1, 0))
        rP, rI = _rope_tables(w0)
        m = dict(shared)
        m.update(xw=np.ascontiguousarray(xw), xo=np.ascontiguousarray(xo), xc=np.ascontiguousarray(ctx[b]), cvT=cvT, mo=mo, ropeP=rP, ropeI=rI)
        maps.append(m)
    return maps, offs


def kernel(**inputs):
    maps, offs = _prep_inputs(inputs)
    if "nc" not in _CACHE:
        _CACHE["nc"] = build_nc()[0]
    nc = _CACHE["nc"]
    res = run_bass_kernel_spmd(nc, maps, core_ids=list(range(8)))
    out = np.zeros((2, 8192, D), np.float32)
    for core in range(8):
        b, j = core // 4, core % 4
        o = res.results[core]["out"]
        out[b, j * 2048:(j + 1) * 2048] = o[offs[core]:offs[core] + 2048]
    return out
```

```python
import numpy as np
from contextlib import ExitStack
import concourse.bass as bass
import concourse.mybir as mybir
from concourse.bass_utils import run_bass_kernel_spmd

F32 = mybir.dt.float32
BF16 = mybir.dt.bfloat16
AF = mybir.ActivationFunctionType
ALU = mybir.AluOpType

D = 1024
NB = 18
NCB = 2
NSB = NB + NCB
NOB = 46
WIN = NB * 128
ALPHA = float((2.0 * 2) ** 0.25)
LN_EPS = 1e-5
RMS_EPS = 1e-6


class Buf:
    __slots__ = ("name", "w", "r", "psum")

    def __init__(self, name, psum=False):
        self.name = name
        self.w = None
        self.r = []
        self.psum = psum


class Tracker:
    ENGS = ("pe", "act", "dve", "pool", "sp")

    def __init__(self, nc, ndma_sems=8, same_engine_waits=True):
        self.nc = nc
        self.same = same_engine_waits
        self.prog = {e: [] for e in self.ENGS}
        self.cnt = {e: 0 for e in self.ENGS}
        self.waited = {e: {} for e in self.ENGS}
        self.ndma = ndma_sems
        self.dma_cnt = {}
        self.dma_rr = {e: 0 for e in self.ENGS}
        self.semkeys = list(self.ENGS)
        for e in ("sp", "pool", "act"):
            for i in range(ndma_sems):
                k = f"d_{e}_{i}"
                self.semkeys.append(k)
                self.dma_cnt[k] = 0
        self.sems = {}
        self.ninstr = 0

    def _deps(self, eng, reads, writes):
        need = {}
        cur = self.cnt[eng]

        def add(ev, is_raw):
            if ev is None:
                return
            k, v = ev
            if k == eng:
                if eng == "pe" or not self.same:
                    return
            if need.get(k, 0) < v:
                need[k] = v
        for b in reads:
            add(b.w, True)
            if b.psum:
                for ev in b.r:
                    if ev[0] != eng:
                        add(ev, False)
        for b in writes:
            add(b.w, False)
            for ev in b.r:
                add(ev, False)
        waits = []
        wd = self.waited[eng]
        for k, v in need.items():
            if wd.get(k, 0) < v:
                wd[k] = v
                waits.append((k, v))
        return waits

    def _commit(self, ev, reads, writes):
        for b in reads:
            b.r.append(ev)
            if len(b.r) > 16:
                m = {}
                for k, v in b.r:
                    if m.get(k, 0) < v:
                        m[k] = v
                b.r = list(m.items())
        for b in writes:
            b.w = ev
            b.r = []

    def op(self, eng, fn, reads=(), writes=()):
        waits = self._deps(eng, reads, writes)
        self.cnt[eng] += 1
        ev = (eng, self.cnt[eng])
        self.prog[eng].append((waits, fn, (eng, 1)))
        self._commit(ev, reads, writes)
        self.ninstr += 1
        return ev

    def dma(self, eng, fn, reads=(), writes=()):
        i = self.dma_rr[eng]
        self.dma_rr[eng] = (i + 1) % self.ndma
        k = f"d_{eng}_{i}"
        waits = self._deps(eng, reads, writes)
        prev = self.dma_cnt[k]
        wd = self.waited[eng]
        if prev > 0 and wd.get(k, 0) < prev * 16:
            wd[k] = prev * 16
            waits.append((k, prev * 16))
        self.dma_cnt[k] = prev + 1
        ev = (k, (prev + 1) * 16)
        self.prog[eng].append((waits, fn, (k, 16)))
        self._commit(ev, reads, writes)
        self.ninstr += 1
        return ev

    def barrier(self):
        evs = [(e, self.cnt[e]) for e in self.ENGS if self.cnt[e] > 0]
        evs += [(k, c * 16) for k, c in self.dma_cnt.items() if c > 0]
        for e in self.ENGS:
            self.wait_all_on(e, [ev for ev in evs if ev[0] != e or e != "pe"])

    def wait_all_on(self, eng, events):
        waits = []
        wd = self.waited[eng]
        for k, v in events:
            if wd.get(k, 0) < v:
                wd[k] = v
                waits.append((k, v))
        if waits:
            self.prog[eng].append((waits, None, None))

    def emit(self, stack):
        nc = self.nc
        for k in self.semkeys:
            self.sems[k] = stack.enter_context(nc.semaphore("s_" + k))
        block = stack.enter_context(nc.Block())
        sems = self.sems

        def run(engname):
            def body(h):
                for waits, fn, inc in self.prog[engname]:
                    for k, v in waits:
                        h.wait_ge(sems[k], v)
                    if fn is not None:
                        ins = fn(h)
                        ins.then_inc(sems[inc[0]], inc[1])
            return body
        block.sync(run("sp"))
        block.scalar(run("act"))
        block.vector(run("dve"))
        block.gpsimd(run("pool"))
        block.tensor(run("pe"))


class Arena:
    def __init__(self, ap):
        self.ap = ap
        self.n = ap.shape[1]
        self.off = 0

    def reset(self, off=0):
        self.off = off

    def f32(self, *shape, parts=128):
        n = int(np.prod(shape))
        a = self.ap[0:parts, self.off:self.off + n]
        self.off += n
        assert self.off <= self.n, ("arena overflow", self.off, self.n)
        self.mx = max(getattr(self, "mx", 0), self.off)
        if len(shape) == 2:
            a = a.rearrange("p (a b) -> p a b", a=shape[0])
        elif len(shape) == 3:
            a = a.rearrange("p (a b c) -> p a b c", a=shape[0], b=shape[1])
        elif len(shape) == 4:
            a = a.rearrange("p (a b c d) -> p a b c d", a=shape[0], b=shape[1], c=shape[2])
        return a

    def bf16(self, *shape, parts=128):
        n = int(np.prod(shape))
        nf = (n + 1) // 2
        a = self.ap[0:parts, self.off:self.off + nf].bitcast(BF16)[:, 0:n]
        self.off += nf
        assert self.off <= self.n, ("arena overflow", self.off, self.n)
        self.mx = max(getattr(self, "mx", 0), self.off)
        if len(shape) == 2:
            a = a.rearrange("p (a b) -> p a b", a=shape[0])
        elif len(shape) == 3:
            a = a.rearrange("p (a b c) -> p a b c", a=shape[0], b=shape[1])
        elif len(shape) == 4:
            a = a.rearrange("p (a b c d) -> p a b c d", a=shape[0], b=shape[1], c=shape[2])
        return a


L1F = 'psevdno'


def build_nc(stop_after=None):
    nc = bass.Bass("TRN2", target_bir_lowering=False)

    def din(name, shape, dt=F32):
        return nc.dram_tensor(name, list(shape), dt, kind="ExternalInput").ap()

    xw = din("xw", [WIN, D]); xo = din("xo", [NOB * 128, D]); xc = din("xc", [256, D])
    cvT = din("cvT", [128, 8, 2]); mo = din("mo", [128, NOB, 2])
    w_mod = din("w_mod", [2, D, 6 * D]); b_mod = din("b_mod", [2, 6 * D])
    ln_g = din("ln_g", [2, 2, D]); ln_b = din("ln_b", [2, 2, D])
    mlp_w1 = din("mlp_w1", [2, D, 4 * D]); mlp_w2 = din("mlp_w2", [2, 4 * D, D])
    w_in = din("w_in", [D, 3104]); theta = din("theta", [128, 2, 256]); wg = din("wg", [33, 512]); gng = din("gng", [128, 1])
    w_out0 = din("w_out0", [D, D]); w_qkv = din("w_qkv", [D, 1536]); sinkc = din("sinkc", [128, 8]); w_out1 = din("w_out1", [D, D])
    ident_d = din("ident", [128, 128]); LC_d = din("LC", [128, 2, 130]); MR_d = din("MR", [128, 2, 128])
    MRF_d = din("MRF", [128, 3, 128]); MK_d = din("MK4", [128, 4, 128]); MW_d = din("MW", [128, 2, 128]); perm_d = din("perm", [128, 128])
    ropeP = din("ropeP", [128, 100]); ropeI = din("ropeI", [128, 4])
    out = nc.dram_tensor("out", [WIN, D], F32, kind="ExternalOutput").ap()
    modscr = nc.dram_tensor("modscr", [2, 2, 6 * D], F32, kind="Internal").ap()
    sbscr = nc.dram_tensor("sbscr", [NSB, 128, 512], BF16, kind="Internal").ap()
    w1b = nc.dram_tensor("w1b", [2, D, 4 * D], BF16, kind="Internal").ap()
    w2b = nc.dram_tensor("w2b", [2, 4 * D, D], BF16, kind="Internal").ap()
    bw1b = [[Buf(f"w1b{l}_{i}") for i in range(8)] for l in range(2)]
    bw2b = [[Buf(f"w2b{l}_{i}") for i in range(32)] for l in range(2)]

    st = ExitStack()
    T = Tracker(nc)

    def sb(name, shape, dt=F32):
        return st.enter_context(nc.sbuf_tensor(name, list(shape), dt))

    xs = sb("xs", [128, NSB, D])
    bxs = [Buf(f"xs{i}") for i in range(NSB)]
    ident = sb("ident_s", [128, 128]); LC = sb("LC_s", [128, 2, 130]); MR = sb("MR_s", [128, 2, 128])
    MK4 = sb("MK4_s", [128, 4, 128]); MW = sb("MW_s", [128, 2, 128]); MRF = sb("MRF_s", [128, 3, 128]); onesf = sb("onesf", [1, 128])
    perm = sb("perm_s", [128, 128], BF16); onesb = sb("onesb", [128, 128], BF16)
    modT = sb("modT", [128, 2, 96]); gngs = sb("gngs", [128, 1]); sinkE = sb("sinkE", [128, 8])
    bconst = Buf("const"); bmodT = Buf("modT")
    NAR = 30280
    ar_t = sb("arena", [128, NAR])
    AR = Arena(ar_t[:, :])
    ps = [st.enter_context(nc.psum_tensor(f"ps{i}", [128, 512], F32)) for i in range(8)]
    bps = [Buf(f"ps{i}", psum=True) for i in range(8)]
    bmodscr = Buf("modscr"); bsbscr = [Buf(f"sbscr{i}") for i in range(NSB)]

    def mm(o, lhsT, rhs, start, stop, R, W, **kw):
        T.op("pe", lambda h: h.matmul(o, lhsT=lhsT, rhs=rhs, start=start, stop=stop, **kw), R, W)

    def act(o, i, func, R, W, **kw):
        T.op("act", lambda h: h.activation(out=o, in_=i, func=func, **kw), R, W)

    def tt(eng, o, a, b, op, R, W):
        T.op(eng, lambda h: h.tensor_tensor(out=o, in0=a, in1=b, op=op), R, W)

    def ts(eng, o, a, s1, s2, op0, op1, R, W):
        if s2 is None:
            T.op(eng, lambda h: h.tensor_scalar(out=o, in0=a, scalar1=s1, scalar2=None, op0=op0), R, W)
        else:
            T.op(eng, lambda h: h.tensor_scalar(out=o, in0=a, scalar1=s1, scalar2=s2, op0=op0, op1=op1), R, W)

    def stt(eng, o, a, s, b, op0, op1, R, W):
        T.op(eng, lambda h: h.scalar_tensor_tensor(out=o, in0=a, scalar=s, in1=b, op0=op0, op1=op1), R, W)

    def cp(eng, o, i, R, W):
        if eng == "act":
            T.op("act", lambda h: h.copy(out=o, in_=i), R, W)
        else:
            T.op(eng, lambda h: h.tensor_copy(out=o, in_=i), R, W)

    def dma(q, o, i, R, W):
        return T.dma(q, lambda h: h.dma_start(out=o, in_=i), R, W)

    def memset(eng, o, v, W):
        T.op(eng, lambda h: h.memset(o, v), (), W)

    dma("sp", ident[:], ident_d, [], [bconst]); dma("sp", LC[:], LC_d, [], [bconst]); dma("sp", MR[:], MR_d, [], [bconst])
    dma("sp", MK4[:], MK_d, [], [bconst]); dma("sp", MRF[:], MRF_d, [], [bconst]); memset("dve", onesf[:], 1.0, [bconst]); dma("sp", MW[:], MW_d, [], [bconst]); dma("pool", perm[:], perm_d, [], [bconst])
    dma("sp", gngs[:], gng, [], [bconst]); dma("sp", sinkE[:], sinkc, [], [bconst])
    memset("dve", onesb[:], 1.0, [bconst])
    act(sinkE[:], sinkE[:], AF.Exp, [bconst], [bconst])
    for i in range(NB):
        dma("sp", xs[:, i, :], xw[i * 128:(i + 1) * 128, :], [], [bxs[i]])
    for i in range(NCB):
        dma("sp", xs[:, NB + i, :], xc[i * 128:(i + 1) * 128, :], [], [bxs[NB + i]])

    AR.reset()
    win = AR.bf16(8, 3104); bwin = Buf("win")
    WIN_OFF = AR.off
    dma("pool", win, w_in.rearrange("(k p) c -> p k c", p=128), [], [bwin])

    AR.reset(WIN_OFF)
    cvs = AR.f32(8, 2); scT = AR.f32(8, 2); mrow = AR.f32(3, 512, parts=2); bmr = AR.f32(3, 512, parts=2)
    NWM = 3
    wm = [AR.f32(8, 512) for _ in range(NWM)]; modR = AR.f32(128, parts=96)
    bcvs = Buf("cvs"); bscT = Buf("scT"); bbm = [Buf(f"bm{i}") for i in range(NWM)]; bmrow = [Buf(f"mrow{i}") for i in range(NWM)]
    bwm = [Buf(f"wm{i}") for i in range(NWM)]; bmodR = Buf("modR")
    dma("sp", cvs, cvT, [], [bcvs])
    act(scT, cvs, AF.Silu, [bcvs], [bscT])
    gi = 0
    for l in range(2):
        wv = w_mod[l].rearrange("(k p) c -> p k c", p=128)
        for cg in range(12):
            wb = gi % NWM; gi += 1
            dma("sp" if gi % 2 == 0 else "pool", wm[wb], wv[:, :, cg * 512:(cg + 1) * 512], [], [bwm[wb]])
            dma("sp", bmr[:, wb, :], b_mod[l, cg * 512:(cg + 1) * 512].partition_broadcast(2), [], [bbm[wb]])
            pb = wb % 2
            for k in range(8):
                mm(ps[pb][0:2, :], scT[:, k, :], wm[wb][:, k, :], k == 0, k == 7, [bscT, bwm[wb]], [bps[pb]])
            tt("dve", mrow[:, wb, :], ps[pb][0:2, :], bmr[:, wb, :], ALU.add, [bps[pb], bbm[wb]], [bmrow[wb]])
            dma("act", modscr[l, :, cg * 512:(cg + 1) * 512], mrow[:, wb, :], [bmrow[wb]], [bmodscr])
        dma("sp", modR, modscr[l].rearrange("r (q p) -> (r q) p", p=128), [bmodscr], [bmodR])
        T.op("pe", lambda h: h.transpose(out=ps[2][:, 0:96], in_=modR, identity=ident[0:96, 0:96]), [bmodR, bconst], [bps[2]])
        cp("dve", modT[:, l, :], ps[2][:, 0:96], [bps[2]], [bmodT])
        for r in range(2):
            for v in (1, 4):
                o = r * 48 + v * 8
                ts("dve", modT[:, l, o:o + 8], modT[:, l, o:o + 8], 1.0, None, ALU.add, None, [bmodT], [bmodT])
    T.barrier()

    def precast_mlp(l):
        for i in range(8):
            dma("pool", w1b[l, i * 128:(i + 1) * 128, :], mlp_w1[l, i * 128:(i + 1) * 128, :], [], [bw1b[l][i]])
        for i in range(32):
            dma("pool", w2b[l, i * 128:(i + 1) * 128, :], mlp_w2[l, i * 128:(i + 1) * 128, :], [], [bw2b[l][i]])

    def make_hT(src, bsrc, dstf, bdst, l, r, which, banks=(6, 7)):
        sidx = r * 48 + (1 if which == 1 else 4) * 8
        bidx = r * 48 + (0 if which == 1 else 3) * 8
        for half in range(2):
            p = ps[banks[half]]; bp = bps[banks[half]]
            for j in range(4):
                k = half * 4 + j
                T.op("pe", lambda h, k=k, j=j, p=p: h.transpose(out=p[:, j * 128:(j + 1) * 128], in_=src[:, k * 128:(k + 1) * 128], identity=ident[:]), [bsrc, bconst], [bp])
            for j in range(4):
                k = half * 4 + j
                if j % 2 == 1:
                    ts("dve", dstf(k), p[:, j * 128:(j + 1) * 128], modT[:, l, sidx + k:sidx + k + 1], modT[:, l, bidx + k:bidx + k + 1],
                       ALU.mult, ALU.add, [bp, bmodT], [bdst])
                else:
                    act(dstf(k), p[:, j * 128:(j + 1) * 128], AF.Identity, [bp, bmodT], [bdst],
                        scale=modT[:, l, sidx + k:sidx + k + 1], bias=modT[:, l, bidx + k:bidx + k + 1])

    def load_gates(l, sub, gt, bgt, cgt=None, bcgt=None):
        v = 2 if sub == 0 else 5
        dma("sp", gt[:, 0, :], modscr[l, 0, v * D:(v + 1) * D].partition_broadcast(128), [bmodscr], [bgt])
        dma("sp", gt[:, 1, :], ln_g[l, sub, :].partition_broadcast(128), [], [bgt])
        dma("sp", gt[:, 2, :], ln_b[l, sub, :].partition_broadcast(128), [], [bgt])
        if cgt is not None:
            dma("sp", cgt, modscr[l, 1, v * D:(v + 1) * D].partition_broadcast(128), [bmodscr], [bcgt])

    def layer_norm(blk, gt, bgt, st6, mv, bst):
        x = xs[:, blk, :]
        b = bxs[blk]
        T.op("dve", lambda h: h.bn_stats(out=st6[:, 0, :], in_=xs[:, blk, 0:512]), [b], [bst])
        T.op("dve", lambda h: h.bn_stats(out=st6[:, 1, :], in_=xs[:, blk, 512:1024]), [b], [bst])
        T.op("dve", lambda h: h.bn_aggr(out=mv[:, 0:2], in_=st6), [bst], [bst])
        act(mv[:, 2:3], mv[:, 1:2], AF.Sqrt, [bst], [bst], bias=LN_EPS, scale=1.0)
        T.op("dve", lambda h: h.reciprocal(out=mv[:, 2:3], in_=mv[:, 2:3]), [bst], [bst])
        ts("dve", x, x, mv[:, 0:1], mv[:, 2:3], ALU.subtract, ALU.mult, [b, bst], [b])
        tt("dve", x, x, gt[:, 1, :], ALU.mult, [b, bgt], [b])
        tt("pool", x, x, gt[:, 2, :], ALU.add, [b, bgt], [b])

    def resid_ln(blk, pa, pb, bpa, bpb, gate, bgate, gt, bgt, tmp, btmp, st6, mv, bst):
        tt("dve", tmp[:, 0:512], pa, gate[:, 0:512], ALU.mult, [bpa, bgate], [btmp])
        tt("dve", tmp[:, 512:1024], pb, gate[:, 512:1024], ALU.mult, [bpb, bgate], [btmp])
        stt("dve", xs[:, blk, :], xs[:, blk, :], ALPHA, tmp, ALU.mult, ALU.add, [bxs[blk], btmp], [bxs[blk]])
        layer_norm(blk, gt, bgt, st6, mv, bst)

    def mlp_phase(l, nblk):
        AR.reset()
        ntok = nblk * 128
        h2T = AR.bf16(8, NSB * 128); bh2T = [Buf(f"h2T{i}") for i in range(nblk)]
        W1g = [AR.bf16(8, 512), AR.bf16(8, 512)]; W2g = [AR.bf16(4, 1024), AR.bf16(4, 1024)]
        bW1 = [Buf("W1a"), Buf("W1b")]; bW2 = [Buf("W2a"), Buf("W2b")]
        aT = [AR.bf16(4, 512), AR.bf16(4, 512)]; baT = [Buf("aTa"), Buf("aTb")]
        rl = [AR.f32(512), AR.f32(512)]; brl = [Buf("rla"), Buf("rlb")]
        gt = AR.f32(3, D); bgt = Buf("gt"); cgt = AR.f32(D); bcgt = Buf("cgt")
        tmpg = [AR.f32(512), AR.f32(512)]; btmpg = [Buf("tga"), Buf("tgb")]
        st6 = AR.f32(2, 6); mv = AR.f32(4); bst = Buf("st")
        load_gates(l, 1, gt, bgt, cgt if nblk > NB else None, bcgt)
        for blk in range(nblk):
            r = 1 if blk >= NB else 0
            make_hT(xs[:, blk, :], bxs[blk], lambda k, blk=blk: h2T[:, k, blk * 128:(blk + 1) * 128], bh2T[blk], l, r, 2)
        for blk in range(nblk):
            T.op("dve", lambda h, blk=blk: h.tensor_scalar(out=xs[:, blk, :], in0=xs[:, blk, :], scalar1=ALPHA, scalar2=None, op0=ALU.mult), [bxs[blk]], [bxs[blk]])
        tiles = []
        t0 = 0
        while t0 < ntok:
            tn = min(512, ntok - t0)
            tiles.append((t0, tn)); t0 += tn
        w1v = w1b[l].rearrange("(k p) f -> p k f", p=128)
        w2v = w2b[l].rearrange("(c p) d -> p c d", p=128)
        cnt = 0

        def load_w(g):
            wb_ = g % 2
            dma("sp", W1g[wb_], w1v[:, :, g * 512:(g + 1) * 512], bw1b[l], [bW1[wb_]])
            dma("sp", W2g[wb_], w2v[:, g * 4:(g + 1) * 4, :], bw2b[l][g * 4:(g + 1) * 4], [bW2[wb_]])
        load_w(0)
        for ffg in range(8):
            wb = ffg % 2
            if ffg + 1 < 8:
                load_w(ffg + 1)
            for ti, (t0, tn) in enumerate(tiles):
                ab = ti % 2
                hb = [bh2T[b] for b in range(t0 // 128, (t0 + tn) // 128)]
                for fc in range(4):
                    pa = ps[fc % 2]; bpa = bps[fc % 2]
                    for k in range(8):
                        mm(pa[:, 0:tn], W1g[wb][:, k, fc * 128:(fc + 1) * 128], h2T[:, k, t0:t0 + tn], k == 0, k == 7, [bW1[wb]] + hb, [bpa])
                    rb = fc % 2
                    act(rl[rb][:, 0:tn], pa[:, 0:tn], AF.Relu, [bpa], [brl[rb]])
                    tt("dve", aT[ab][:, fc, 0:tn], rl[rb][:, 0:tn], rl[rb][:, 0:tn], ALU.mult, [brl[rb]], [baT[ab]])
                for tb in range(tn // 128):
                    blk = t0 // 128 + tb
                    gate = cgt if blk >= NB else gt[:, 0, :]
                    bg = bcgt if blk >= NB else bgt
                    for dh in range(2):
                        pi = 2 + (cnt % 4); cnt += 1
                        po = ps[pi]; bpo = bps[pi]
                        for fc in range(4):
                            mm(po[:, :], aT[ab][:, fc, tb * 128:(tb + 1) * 128], W2g[wb][:, fc, dh * 512:(dh + 1) * 512], fc == 0, fc == 3, [baT[ab], bW2[wb]], [bpo])
                        tg = cnt % 2
                        tt("dve", tmpg[tg], po[:, :], gate[:, dh * 512:(dh + 1) * 512], ALU.mult, [bpo, bg], [btmpg[tg]])
                        tt("pool", xs[:, blk, dh * 512:(dh + 1) * 512], xs[:, blk, dh * 512:(dh + 1) * 512], tmpg[tg], ALU.add, [bxs[blk], btmpg[tg]], [bxs[blk]])
        for blk in range(nblk):
            layer_norm(blk, gt, bgt, st6, mv, bst)
        T.barrier()

    def layer0_mixer():
        AR.reset(WIN_OFF)
        hT = AR.bf16(8, 128); bhT = Buf("hT")
        ktok = AR.bf16(512); bktok = Buf("ktok")
        V = AR.bf16(1024); bV = Buf("V")
        Kend = AR.bf16(2, 512); bKend = Buf("Kend")
        sp = AR.f32(2, 512); bsp = Buf("sp")
        scrA = AR.f32(2, 512); bscrA = Buf("scrA")
        scrB = AR.f32(2, 512); bscrB = Buf("scrB")
        lrT = AR.bf16(128, parts=33); blrT = Buf("lrT")
        Wg = AR.bf16(512, parts=33); bWg = Buf("Wg")
        dec = AR.f32(2, 4, 2); bdec = Buf("dec")
        Sf = [AR.f32(4, 128), AR.f32(4, 128)]; bSf = [Buf("Sf0"), Buf("Sf1")]
        off_shared = AR.off
        Sbacc = AR.f32(4, 128); bSbacc = Buf("Sbacc")
        ecf = AR.f32(2, 4); becf = Buf("ecf")
        xin = [AR.f32(D), AR.f32(D)]; bxin = [Buf("xin0"), Buf("xin1")]
        mos = AR.f32(NOB, 2); bmos = Buf("mos")
        thr = scrB[:, :, 0:256]; bthr = bscrB
        dma("pool", Wg, wg, [], [bWg])
        precast_mlp(0)
        precast_mlp(1)
        dma("sp", mos, mo, [], [bmos])
        dma("sp", thr, theta, [], [bthr])
        act(thr, thr, AF.Exp, [bthr], [bthr])
        act(thr, thr, AF.Ln, [bthr], [bthr], scale=-1.0, bias=1.0)
        for d in range(2):
            ts("dve", sp[:, d, 0:256], thr[:, d, :], -16.0, None, ALU.mult, None, [bthr], [bsp])
        memset("dve", lrT[32:33, :], 1.0, [blrT])
        for t_, b_ in ((Sf[0], bSf[0]), (Sf[1], bSf[1]), (Sbacc, bSbacc)):
            memset("dve", t_, 0.0, [b_])

        def kv_front(src, bsrc, r, mask_blk):
            make_hT(src, bsrc, lambda k: hT[:, k, :], bhT, 0, r, 1, banks=(0, 1))
            for k in range(8):
                mm(ps[0][:, :], hT[:, k, :], win[:, k, 512:1024], k == 0, k == 7, [bhT, bwin], [bps[0]])
            for k in range(8):
                mm(ps[1][:, :], hT[:, k, :], win[:, k, 1024:1536], k == 0, k == 7, [bhT, bwin], [bps[1]])
            for k in range(8):
                mm(ps[2][:, :], hT[:, k, :], win[:, k, 1536:2048], k == 0, k == 7, [bhT, bwin], [bps[2]])
            for k in range(8):
                mm(ps[3][0:32, 0:128], win[:, k, 3072:3104], hT[:, k, :], k == 0, k == 7, [bhT, bwin], [bps[3]])
            cp("act", ktok, ps[0][:, :], [bps[0]], [bktok])
            cp("dve", V[:, 0:512], ps[1][:, :], [bps[1]], [bV])
            cp("dve", V[:, 512:1024], ps[2][:, :], [bps[2]], [bV])
            cp("act", lrT[0:32, :], ps[3][0:32, 0:128], [bps[3]], [blrT])
            mm(ps[3][:, :], lrT[0:33, :], Wg[0:33, :], True, True, [blrT, bWg], [bps[3]])
            act(scrB.rearrange("p a b -> p (a b)")[:, 0:512], ps[3][:, :], AF.Exp, [bps[3]], [bscrB], scale=-1.0)
            act(sp[:, :, 256:512], scrB.rearrange("p a b -> p (a b)")[:, 0:512].rearrange("p (a b) -> p a b", a=2), AF.Ln, [bscrB], [bsp], bias=1.0, scale=1.0)
            if mask_blk is None:
                spm = sp; bspm = bsp
            else:
                spm = scrA; bspm = bscrA
                for d in range(2):
                    ts("dve", scrA[:, d, :], sp[:, d, :], mos[:, mask_blk, d:d + 1], None, ALU.mult, None, [bsp, bmos], [bscrA])
            for d in range(2):
                mm(ps[4 + d][:, :], MR[:, d, :], spm[:, d, :], True, True, [bconst, bspm], [bps[4 + d]])
            for d in range(2):
                act(scrB[:, d, :], ps[4 + d][:, :], AF.Exp, [bps[4 + d]], [bscrB])
            for d in range(2):
                if mask_blk is None:
                    tt("dve", Kend[:, d, :], ktok, scrB[:, d, :], ALU.mult, [bktok, bscrB], [bKend])
                else:
                    stt("dve", Kend[:, d, :], ktok, mos[:, mask_blk, d:d + 1], scrB[:, d, :], ALU.mult, ALU.mult, [bktok, bscrB, bmos], [bKend])
            return spm, bspm

        def kv_mms():
            bank = {(0, 0): 2, (0, 1): 4, (1, 0): 0, (1, 1): 6}
            for d in range(2):
                for hp in range(4):
                    for c in range(2):
                        bi = bank[(d, c)] + hp // 2
                        mm(ps[bi][:, (hp % 2) * 256:(hp % 2) * 256 + 256], Kend[64 * c:64 * c + 64, d, hp * 128:(hp + 1) * 128],
                           V[64 * c:64 * c + 64, hp * 256:(hp + 1) * 256], True, True, [bKend, bV], [bps[bi]])
            return bank

        def kv_ap(bank, d, c, hp, half):
            bi = bank[(d, c)] + hp // 2
            col = (hp % 2) * 256 + half * 128
            return ps[bi][64 * half:64 * half + 64, col:col + 128], bps[bi]

        def btot_dec(spm, bspm, pbank):
            for d in range(2):
                for hp in range(4):
                    o = (d * 4 + hp) * 2
                    mm(ps[pbank][:, o:o + 2], spm[:, d, hp * 128:(hp + 1) * 128], LC[:, 0, 128:130], True, True, [bspm, bconst], [bps[pbank]])
            act(dec.rearrange("p a b c -> p (a b c)"), ps[pbank][:, 0:16], AF.Exp, [bps[pbank]], [bdec])

        kvs_off = AR.off
        KVs = AR.bf16(NSB, 4, 128); bKVs = [Buf(f"KVs{i}") for i in range(NSB)]
        Dst = AR.f32(NSB, 4); bDst = Buf("Dst")
        sstage = [AR.bf16(4, 128), AR.bf16(4, 128)]; bsstage = [Buf("sst0"), Buf("sst1")]
        Srun = [AR.f32(4, 128), AR.f32(4, 128)]; bSrun = [Buf("Srun0"), Buf("Srun1")]
        Sctxb = AR.f32(4, 128); bSctxb = Buf("Sctxb")
        cur = [0]

        def passA_block(src, bsrc, r, mask_blk, fwd_state, bwd_mode, store_idx):
            spm, bspm = kv_front(src, bsrc, r, mask_blk)
            btot_dec(spm, bspm, 3)
            bank = kv_mms()
            if fwd_state:
                for c in range(2):
                    a = cur[0]; b = 1 - a
                    for hp in range(4):
                        for half in range(2):
                            rows = slice(64 * half, 64 * half + 64)
                            kv, bkv = kv_ap(bank, 0, c, hp, half)
                            stt("dve", Sf[b][rows, hp, :], Sf[a][rows, hp, :], dec[rows, 0, hp, c:c + 1], kv, ALU.mult, ALU.add, [bSf[a], bdec, bkv], [bSf[b]])
                    cur[0] = b
            if bwd_mode == "ptrick":
                raise AssertionError("unused")
            else:
                i = store_idx
                for hp in range(4):
                    for half in range(2):
                        rows = slice(64 * half, 64 * half + 64)
                        kv0, bkv0 = kv_ap(bank, 1, 0, hp, half)
                        kv1, bkv1 = kv_ap(bank, 1, 1, hp, half)
                        cp("act", KVs[rows, i, hp, :], kv0, [bkv0], [bKVs[i]])
                        stt("dve", KVs[rows, i, hp, :], kv1, dec[rows, 1, hp, 0:1], KVs[rows, i, hp, :], ALU.mult, ALU.add, [bkv1, bdec, bKVs[i]], [bKVs[i]])
                tt("dve", Dst[:, i, :], dec[:, 1, :, 0], dec[:, 1, :, 1], ALU.mult, [bdec], [bDst])

        def reverse_scan(S0, bS0, blocks):
            cpidx = 0
            cp("dve", Srun[0], S0, [bS0], [bSrun[0]])
            a = 0
            for n, i in enumerate(blocks):
                sbuf_ = sstage[n % 2]; bsb = bsstage[n % 2]
                cp("act", sbuf_, Srun[a], [bSrun[a]], [bsb])
                dma("sp", sbscr[i].rearrange("p (h e) -> p h e", h=4), sbuf_, [bsb], [bsbscr[i]])
                b = 1 - a
                for hp in range(4):
                    stt("dve", Srun[b][:, hp, :], Srun[a][:, hp, :], Dst[:, i, hp:hp + 1], KVs[:, i, hp, :], ALU.mult, ALU.add, [bSrun[a], bDst, bKVs[i]], [bSrun[b]])
                a = b
            return a

        for cb in range(NCB):
            passA_block(xs[:, NB + cb, :], bxs[NB + cb], 1, None, True, "store", NB + cb)
        memset("dve", Sctxb, 0.0, [bSctxb])
        a = reverse_scan(Sctxb, bSctxb, [NB + 1, NB])
        cp("dve", Sctxb, Srun[a], [bSrun[a]], [bSctxb])
        T.barrier()
        sub = Arena(ar_t[:, kvs_off:kvs_off + 5120])
        hT2 = [hT, sub.bf16(8, 128)]; bhT2 = [bhT, Buf("hT2")]
        ktok2 = [ktok, sub.bf16(512)]; bktok2 = [bktok, Buf("ktok2")]
        V2 = [V, sub.bf16(1024)]; bV2 = [bV, Buf("V2")]
        Kend2 = [Kend, sub.bf16(2, 512)]; bKend2 = [bKend, Buf("Kend2")]
        spm2 = [scrA, sub.f32(2, 512)]; bspm2 = [bscrA, Buf("spm2")]
        eb2 = [scrB, sub.f32(2, 512)]; beb2 = [bscrB, Buf("eb2")]
        crow = sub.f32(2, 512, parts=1); bcrow = Buf("crow")
        memset("dve", crow, 0.0, [bcrow])
        for bi in range(4, 8):
            memset("dve", ps[bi][:, :], 0.0, [bps[bi]])
        def o_front(ob):
            p = ob % 2
            hTp, bhTp = hT2[p], bhT2[p]
            dma("sp", xin[p], xo[ob * 128:(ob + 1) * 128, :], [], [bxin[p]])
            make_hT(xin[p], bxin[p], lambda k, hTp=hTp: hTp[:, k, :], bhTp, 0, 0, 1, banks=(0, 1))
            for k in range(8):
                mm(ps[2][:, :], hTp[:, k, :], win[:, k, 512:1024], k == 0, k == 7, [bhTp, bwin], [bps[2]])
            for k in range(8):
                mm(ps[3][:, :], hTp[:, k, :], win[:, k, 1024:1536], k == 0, k == 7, [bhTp, bwin], [bps[3]])
            for k in range(8):
                mm(ps[0][:, :], hTp[:, k, :], win[:, k, 1536:2048], k == 0, k == 7, [bhTp, bwin], [bps[0]])
            for k in range(8):
                mm(ps[1][0:32, 0:128], win[:, k, 3072:3104], hTp[:, k, :], k == 0, k == 7, [bhTp, bwin], [bps[1]])
            cp("act", ktok2[p], ps[2][:, :], [bps[2]], [bktok2[p]])
            cp("dve", V2[p][:, 0:512], ps[3][:, :], [bps[3]], [bV2[p]])
            cp("dve", V2[p][:, 512:1024], ps[0][:, :], [bps[0]], [bV2[p]])
            cp("act", lrT[0:32, :], ps[1][0:32, 0:128], [bps[1]], [blrT])
            mm(ps[1][:, :], lrT[0:33, :], Wg[0:33, :], True, True, [blrT, bWg], [bps[1]])
            ebf = eb2[p].rearrange("p a b -> p (a b)")
            act(ebf[:, 0:512], ps[1][:, :], AF.Exp, [bps[1]], [beb2[p]], scale=-1.0)
            act(sp[:, :, 256:512], ebf[:, 0:512].rearrange("p (a b) -> p a b", a=2), AF.Ln, [beb2[p]], [bsp], bias=1.0, scale=1.0)
            for d in range(2):
                ts("dve", spm2[p][:, d, :], sp[:, d, :], mos[:, ob, d:d + 1], None, ALU.mult, None, [bsp, bmos], [bspm2[p]])

        def o_gates(ob):
            p = ob % 2
            for d in range(2):
                mm(ps[2 + d][:, :], MRF[:, d, :], spm2[p][:, d, :], True, False, [bconst, bspm2[p]], [bps[2 + d]])
                mm(ps[2 + d][:, :], onesf[0:1, :], crow[0:1, d, :], False, True, [bconst, bcrow], [bps[2 + d]])
            for d in range(2):
                mm(ps[0][32 * d:32 * d + 1, :], MRF[:, 2, 0:1], spm2[p][:, d, :], True, True, [bconst, bspm2[p]], [bps[0]])
            for d in range(2):
                act(eb2[p][:, d, :], ps[2 + d][:, :], AF.Exp, [bps[2 + d]], [beb2[p]])
            for d in range(2):
                stt("dve", Kend2[p][:, d, :], ktok2[p], mos[:, ob, d:d + 1], eb2[p][:, d, :], ALU.mult, ALU.mult, [bktok2[p], beb2[p], bmos], [bKend2[p]])
            for d in range(2):
                tt("dve", crow[0:1, d, :], crow[0:1, d, :], ps[0][32 * d:32 * d + 1, :], ALU.add, [bcrow, bps[0]], [bcrow])

        def o_kv(ob):
            p = ob % 2
            for d in range(2):
                for hp in range(4):
                    bi = 4 + 2 * d + hp // 2
                    mm(ps[bi][:, (hp % 2) * 256:(hp % 2) * 256 + 256], Kend2[p][:, d, hp * 128:(hp + 1) * 128], V2[p][:, hp * 256:(hp + 1) * 256],
                       False, False, [bKend2[p], bV2[p]], [bps[bi]], skip_group_check=True)

        o_front(0)
        for ob in range(NOB):
            o_gates(ob)
            if ob + 1 < NOB:
                o_front(ob + 1)
            o_kv(ob)
        for d in range(2):
            for hp in range(4):
                mm(ps[0][:, d * 4 + hp:d * 4 + hp + 1], crow[0:1, d, hp * 128:(hp + 1) * 128], onesf[0:1, 0:1], True, True, [bcrow, bconst], [bps[0]])
        act(ecf.rearrange("p a b -> p (a b)"), ps[0][:, 0:8], AF.Exp, [bps[0]], [becf])
        a = cur[0]; b = 1 - a
        for hp in range(4):
            for half in range(2):
                rows = slice(64 * half, 64 * half + 64)
                col = (hp % 2) * 256 + half * 128
                stt("dve", Sf[b][rows, hp, :], Sf[a][rows, hp, :], ecf[rows, 0, hp:hp + 1], ps[4 + hp // 2][rows, col:col + 128], ALU.mult, ALU.add,
                    [bSf[a], becf, bps[4 + hp // 2]], [bSf[b]])
                stt("dve", Sbacc[rows, hp, :], Sctxb[rows, hp, :], ecf[rows, 1, hp:hp + 1], ps[6 + hp // 2][rows, col:col + 128], ALU.mult, ALU.add,
                    [bSctxb, becf, bps[6 + hp // 2]], [bSbacc])
        cur[0] = b
        T.barrier()
        for blk in range(NB):
            passA_block(xs[:, blk, :], bxs[blk], 0, None, False, "store", blk)
        reverse_scan(Sbacc, bSbacc, list(range(NB - 1, -1, -1)))
        T.barrier()

        AR.reset(off_shared)
        wout = AR.bf16(8, D); bwout = Buf("wout")
        dma("pool", wout, w_out0.rearrange("(k p) c -> p k c", p=128), [], [bwout])
        gt = AR.f32(3, D); bgt = Buf("gt")
        load_gates(0, 0, gt, bgt)
        qkT = AR.bf16(2, 4, 128); bqkT = Buf("qkT")
        sg = AR.bf16(8, 128); bsg = Buf("sg")
        QK = AR.bf16(2, 2, 4, 128); bQK = Buf("QK")
        PT = AR.bf16(2, 512); bPT = [Buf("PTa"), Buf("PTb")]
        Sbf = AR.bf16(2, 2, 4, 128); bSbf = Buf("Sbf")
        osq = PT
        yT = AR.bf16(8, 128); byT = Buf("yT")
        st6 = AR.f32(2, 6); mv = AR.f32(4); bst = Buf("st")

        def passB_block(blk, r, gate, bgate):
            src = xs[:, blk, :]; bsrc = bxs[blk]
            dma("sp", Sbf[:, 1, 0, :, :], sbscr[blk].rearrange("p (h e) -> p h e", h=4), [bsbscr[blk]], [bSbf])
            spm, bspm = kv_front(src, bsrc, r, None)
            qcols = [0, 128, 256, 384]; kcols = [512, 640, 768, 896]
            for hp in range(4):
                for k in range(8):
                    mm(ps[0][:, hp * 128:(hp + 1) * 128], win[:, k, qcols[hp]:qcols[hp] + 128], hT[:, k, :], k == 0, k == 7, [bwin, bhT], [bps[0]])
            for hp in range(4):
                for k in range(8):
                    mm(ps[1][:, hp * 128:(hp + 1) * 128], win[:, k, kcols[hp]:kcols[hp] + 128], hT[:, k, :], k == 0, k == 7, [bwin, bhT], [bps[1]])
            for i in range(4):
                for k in range(8):
                    mm(ps[2][:, i * 128:(i + 1) * 128], win[:, k, 2048 + i * 128:2048 + (i + 1) * 128], hT[:, k, :], k == 0, k == 7, [bwin, bhT], [bps[2]])
            for i in range(4):
                for k in range(8):
                    mm(ps[3][:, i * 128:(i + 1) * 128], win[:, k, 2560 + i * 128:2560 + (i + 1) * 128], hT[:, k, :], k == 0, k == 7, [bwin, bhT], [bps[3]])
            T.op("act", lambda h: h.mul(out=qkT[:, 0, :, :].rearrange("p a b -> p (a b)"), in_=ps[0][:, :], mul=0.125), [bps[0]], [bqkT])
            cp("dve", qkT[:, 1, :, :].rearrange("p a b -> p (a b)"), ps[1][:, :], [bps[1]], [bqkT])
            act(sg[:, 0:4, :].rearrange("p a b -> p (a b)"), ps[2][:, :], AF.Silu, [bps[2]], [bsg])
            act(sg[:, 4:8, :].rearrange("p a b -> p (a b)"), ps[3][:, :], AF.Silu, [bps[3]], [bsg])
            for d in range(2):
                for hp in range(4):
                    mm(ps[d][:, hp * 128:(hp + 1) * 128], sp[:, d, hp * 128:(hp + 1) * 128], LC[:, d, 0:128], True, True, [bsp, bconst], [bps[d]])
            btot_dec(sp, bsp, 2)
            for d in range(2):
                act(scrA.rearrange("p a b -> p (a b)")[:, 0:512], ps[d][:, :], AF.Exp, [bps[d]], [bscrA])
                tt("dve", QK[:, d, 0, :, :].rearrange("p a b -> p (a b)"), qkT[:, 0, :, :].rearrange("p a b -> p (a b)"), scrA.rearrange("p a b -> p (a b)")[:, 0:512], ALU.mult, [bqkT, bscrA], [bQK])
                act(scrA.rearrange("p a b -> p (a b)")[:, 512:1024], ps[d][:, :], AF.Exp, [bps[d]], [bscrA], scale=-1.0)
                tt("dve", QK[:, d, 1, :, :].rearrange("p a b -> p (a b)"), qkT[:, 1, :, :].rearrange("p a b -> p (a b)"), scrA.rearrange("p a b -> p (a b)")[:, 512:1024], ALU.mult, [bqkT, bscrA], [bQK])
            bank = kv_mms()
            a = cur[0]; b = 1 - a
            cp("act", Sbf[:, 0, 0, :, :], Sf[a], [bSf[a]], [bSbf])
            for hp in range(4):
                for half in range(2):
                    rows = slice(64 * half, 64 * half + 64)
                    kv, bkv = kv_ap(bank, 0, 0, hp, half)
                    stt("dve", Sf[b][rows, hp, :], Sf[a][rows, hp, :], dec[rows, 0, hp, 0:1], kv, ALU.mult, ALU.add, [bSf[a], bdec, bkv], [bSf[b]])
            cp("act", Sbf[:, 0, 1, :, :], Sf[b], [bSf[b]], [bSbf])
            for hp in range(4):
                for half in range(2):
                    rows = slice(64 * half, 64 * half + 64)
                    kv, bkv = kv_ap(bank, 0, 1, hp, half)
                    stt("dve", Sf[a][rows, hp, :], Sf[b][rows, hp, :], dec[rows, 0, hp, 1:2], kv, ALU.mult, ALU.add, [bSf[b], bdec, bkv], [bSf[a]])
            for hp in range(4):
                for half in range(2):
                    rows = slice(64 * half, 64 * half + 64)
                    kv, bkv = kv_ap(bank, 1, 1, hp, half)
                    stt("dve", Sbf[rows, 1, 1, hp, :], Sbf[rows, 1, 0, hp, :], dec[rows, 1, hp, 1:2], kv, ALU.mult, ALU.add, [bSbf, bdec, bkv], [bSbf])
            for hpp in range(2):
                for hl in range(2):
                    hp = hpp * 2 + hl
                    for d in range(2):
                        for X in range(2):
                            rows = slice(64 * X, 64 * X + 64)
                            col = (hl * 2 + d) * 128
                            mm(ps[X][:, col:col + 128], QK[rows, d, 1, hp, :], QK[rows, d, 0, hp, :], True, True, [bQK], [bps[X]])
                for X in range(2):
                    tt("dve", PT[:, X, :], ps[X][:, :], MK4.rearrange("p a b -> p (a b)"), ALU.mult, [bps[X], bconst], [bPT[X]])
                for hl in range(2):
                    hp = hpp * 2 + hl
                    for X in range(2):
                        rows = slice(64 * X, 64 * X + 64)
                        o = ps[2 + X][:, hp * 128:(hp + 1) * 128]; bo = bps[2 + X]
                        vv = V[:, hp * 256 + X * 128:hp * 256 + X * 128 + 128]
                        R = [bV, bPT[X], bSbf, bQK]
                        mm(o, vv, PT[:, X, (hl * 2 + 0) * 128:(hl * 2 + 1) * 128], True, False, R, [bo])
                        mm(o, vv, PT[:, X, (hl * 2 + 1) * 128:(hl * 2 + 2) * 128], False, False, R, [bo])
                        mm(o[:, 0:64], Sbf[rows, 0, 0, hp, :], QK[rows, 0, 0, hp, 0:64], False, False, R, [bo])
                        mm(o[:, 64:128], Sbf[rows, 0, 1, hp, :], QK[rows, 0, 0, hp, 64:128], False, False, R, [bo])
                        mm(o[:, 64:128], Sbf[rows, 1, 0, hp, :], QK[rows, 1, 0, hp, 64:128], False, False, R, [bo])
                        mm(o[:, 0:64], Sbf[rows, 1, 1, hp, :], QK[rows, 1, 0, hp, 0:64], False, True, R, [bo])
            for X in range(2):
                act(osq[:, X, :], ps[2 + X][:, :], AF.Square, [bps[2 + X]], [bPT[X]])
            for X in range(2):
                mm(ps[4 + X][:, :], onesb[:, :], osq[:, X, :], True, True, [bconst, bPT[X]], [bps[4 + X]])
            for X in range(2):
                act(scrA[:, X, :], ps[4 + X][:, :], AF.Sqrt, [bps[4 + X]], [bscrA], scale=1.0 / 128.0, bias=RMS_EPS)
            T.op("dve", lambda h: h.reciprocal(out=scrA.rearrange("p a b -> p (a b)"), in_=scrA.rearrange("p a b -> p (a b)")), [bscrA], [bscrA])
            for X in range(2):
                tt("dve", scrB[:, X, 0:256], ps[2 + X][:, 0:256], scrA[:, X, 0:256], ALU.mult, [bps[2 + X], bscrA], [bscrB])
                stt("dve", scrB[:, X, 256:512], ps[2 + X][:, 256:512], gngs[:, 0:1], scrA[:, X, 256:512], ALU.mult, ALU.mult, [bps[2 + X], bscrA, bconst], [bscrB])
                tt("dve", yT.rearrange("p (h x) t -> p h x t", x=2)[:, :, X, :], scrB[:, X, :].rearrange("p (h t) -> p h t", h=4),
                   sg.rearrange("p (h x) t -> p h x t", x=2)[:, :, X, :], ALU.mult, [bscrB, bsg], [byT])
            for dh in range(2):
                for fc in range(8):
                    mm(ps[6 + dh][:, :], yT[:, fc, :], wout[:, fc, dh * 512:(dh + 1) * 512], fc == 0, fc == 7, [byT, bwout], [bps[6 + dh]])
            resid_ln(blk, ps[6][:, :], ps[7][:, :], bps[6], bps[7], gate, bgate, gt, bgt,
                     scrA.rearrange("p a b -> p (a b)"), bscrA, st6, mv, bst)

        for blk in range(NB):
            passB_block(blk, 0, gt[:, 0, :], bgt)
        dma("sp", gt[:, 0, :], modscr[0, 1, 2 * D:3 * D].partition_broadcast(128), [bmodscr], [bgt])
        memset("dve", Sf[cur[0]], 0.0, [bSf[cur[0]]])
        for cb in range(NCB):
            passB_block(NB + cb, 1, gt[:, 0, :], bgt)
        T.barrier()

    def layer1_mixer():
        AR.reset()
        wq = AR.bf16(8, 1536); bwq = Buf("wq")
        dma("pool", wq, w_qkv.rearrange("(k p) c -> p k c", p=128), [], [bwq])
        wout = AR.bf16(8, D); bwout = Buf("wout")
        dma("pool", wout, w_out1.rearrange("(k p) c -> p k c", p=128), [], [bwout])
        gt = AR.f32(3, D); bgt = Buf("gt")
        load_gates(1, 0, gt, bgt)
        kT = AR.bf16(2, WIN); bkT = [Buf(f"kT{i}") for i in range(NB)]
        Vw = AR.bf16(NB, 256); bVw = [Buf(f"Vw{i}") for i in range(NB)]
        kTc = AR.bf16(2, 256); Vc = AR.bf16(2, 256); bctx = Buf("ctxkv")
        hT = [AR.bf16(8, 128), AR.bf16(8, 128)]; bhT = [Buf("hTa"), Buf("hTb")]
        qT = [AR.bf16(8, 128), AR.bf16(8, 128)]; bqT = [Buf("qTa"), Buf("qTb")]
        tmp = AR.f32(D); btmp = Buf("tmp")
        cosT = AR.f32(WIN); sinT = AR.f32(WIN); ridx = AR.f32(4); invf = AR.f32(1); brope = Buf("rope")
        cc = AR.f32(100); sc_ = AR.f32(100); tc_ = AR.f32(100); bcmp = Buf("ropec")
        dma("sp", cc, ropeP, [], [bcmp])
        dma("sp", ridx, ropeI, [], [bcmp])
        act(invf, ridx[:, 0:1], AF.Exp, [bcmp], [bcmp], scale=-float(np.log(10000.0)) / 16.0)
        ts("dve", cc, cc, invf[:, 0:1], None, ALU.mult, None, [bcmp], [bcmp])
        act(sc_, cc, AF.Sin, [bcmp], [bcmp], scale=1.0 / 128.0)
        act(cc, cc, AF.Sin, [bcmp], [bcmp], scale=1.0 / 256.0)
        tt("dve", cc, cc, cc, ALU.mult, [bcmp], [bcmp])
        ts("dve", cc, cc, -2.0, 1.0, ALU.mult, ALU.add, [bcmp], [bcmp])
        for _ in range(7):
            tt("dve", tc_, sc_, sc_, ALU.mult, [bcmp], [bcmp])
            stt("dve", sc_, sc_, 2.0, cc, ALU.mult, ALU.mult, [bcmp], [bcmp])
            ts("dve", cc, tc_, -2.0, 1.0, ALU.mult, ALU.add, [bcmp], [bcmp])
        ts("dve", sc_, sc_, ridx[:, 1:2], None, ALU.mult, None, [bcmp], [bcmp])
        for full, cmpt in ((cosT, cc), (sinT, sc_)):
            f3 = full.rearrange("p (r c) -> p r c", r=36)
            aR = cmpt[:, 0:36]; aC = cmpt[:, 36:100]
            bR = bass.AP(aR.tensor, aR.offset, [list(aR.ap[0]), [1, 36], [0, 64]])
            bC = bass.AP(aC.tensor, aC.offset, [list(aC.ap[0]), [0, 36], [1, 64]])
            ts("dve", f3, bR, ridx[:, 2:3], None, ALU.mult, None, [bcmp], [brope])
            stt("dve", f3, bC, ridx[:, 3:4], f3, ALU.mult, ALU.add, [bcmp, brope], [brope])

        def bc4(t, blk, nch):
            a = t[:, blk * 128:(blk + 1) * 128]
            return bass.AP(a.tensor, a.offset, [list(a.ap[0]), [0, nch], [1, 128]])
        tb16 = AR.bf16(512); btb = Buf("tb16")
        r1 = AR.f32(512); r2 = AR.f32(512); br1 = Buf("r1"); br2 = Buf("r2")
        E = AR.bf16(2, 640); bE = [Buf("Ea"), Buf("Eb")]
        rd = AR.f32(128); brd = Buf("rd")
        attT = AR.bf16(8, 128); battT = Buf("attT")
        st6 = AR.f32(2, 6); mv = AR.f32(4); bst = Buf("st")

        for cb in range(NCB):
            make_hT(xs[:, NB + cb, :], bxs[NB + cb], lambda k: hT[0][:, k, :], bhT[0], 1, 1, 1)
            for m in range(2):
                for k in range(8):
                    mm(ps[0][:, m * 128:(m + 1) * 128], wq[:, k, 1024 + m * 128:1024 + (m + 1) * 128], hT[0][:, k, :], k == 0, k == 7, [bwq, bhT[0]], [bps[0]])
            for m in range(2):
                cp("act", kTc[:, m, cb * 128:(cb + 1) * 128], ps[0][:, m * 128:(m + 1) * 128], [bps[0]], [bctx])
            for k in range(8):
                mm(ps[1][:, 0:256], hT[0][:, k, :], wq[:, k, 1280:1536], k == 0, k == 7, [bwq, bhT[0]], [bps[1]])
            cp("dve", Vc[:, cb, :], ps[1][:, 0:256], [bps[1]], [bctx])

        def rope(pbank, ncol, blk, outap, bout, nch):
            v3 = lambda t: t[:, 0:ncol].rearrange("p (a b) -> p a b", a=nch)
            cp("act", tb16[:, 0:ncol], ps[pbank][:, 0:ncol], [bps[pbank]], [btb])
            mm(ps[5][:, 0:ncol], perm[:, :], tb16[:, 0:ncol], True, True, [bconst, btb], [bps[5]])
            tt("dve", v3(r1), v3(ps[pbank]), bc4(cosT, blk, nch), ALU.mult, [bps[pbank], brope], [br1])
            tt("dve", v3(r2), v3(ps[5]), bc4(sinT, blk, nch), ALU.mult, [bps[5], brope], [br2])
            tt("dve", outap, v3(r1), v3(r2), ALU.add, [br1, br2], [bout])

        def proj_block(blk):
            hb = blk % 2
            make_hT(xs[:, blk, :], bxs[blk], lambda k: hT[hb][:, k, :], bhT[hb], 1, 0, 1)
            for m in range(2):
                for k in range(8):
                    mm(ps[0][:, m * 128:(m + 1) * 128], wq[:, k, 1024 + m * 128:1024 + (m + 1) * 128], hT[hb][:, k, :], k == 0, k == 7, [bwq, bhT[hb]], [bps[0]])
            for k in range(8):
                mm(ps[1][:, 0:256], hT[hb][:, k, :], wq[:, k, 1280:1536], k == 0, k == 7, [bwq, bhT[hb]], [bps[1]])
            cp("act", Vw[:, blk, :], ps[1][:, 0:256], [bps[1]], [bVw[blk]])
            rope(0, 256, blk, kT[:, :, blk * 128:(blk + 1) * 128], bkT[blk], 2)
            for half in range(2):
                for j in range(4):
                    c = half * 4 + j
                    for k in range(8):
                        mm(ps[2 + half][:, j * 128:(j + 1) * 128], wq[:, k, c * 128:(c + 1) * 128], hT[hb][:, k, :], k == 0, k == 7, [bwq, bhT[hb]], [bps[2 + half]])
                rope(2 + half, 512, blk, qT[hb][:, half * 4:half * 4 + 4, :], bqT[hb], 4)

        def attn_block(qb):
            hb = qb % 2
            loc = []
            if qb > 0:
                loc.append((qb - 1, 0))
            loc.append((qb, None))
            if qb < NB - 1:
                loc.append((qb + 1, 1))
            nl = len(loc)
            for j in range(8):
                m = j // 4
                for X in range(2 if 's' in L1F else 0):
                    rows = slice(64 * X, 64 * X + 64)
                    for li, (kb, _) in enumerate(loc):
                        mm(ps[X][:, li * 128:(li + 1) * 128], kT[rows, m, kb * 128:(kb + 1) * 128], qT[hb][rows, j, :], True, True, [bkT[kb], bqT[hb]], [bps[X]])
                for X in range(2 if 's' in L1F else 0):
                    rows = slice(64 * X, 64 * X + 64)
                    for cb in range(NCB):
                        mm(ps[2 + X][:, cb * 128:(cb + 1) * 128], kTc[rows, m, cb * 128:(cb + 1) * 128], qT[hb][rows, j, :], True, True, [bctx, bqT[hb]], [bps[2 + X]])
                for X in range(2 if 'e' in L1F else 0):
                    act(E[:, X, 0:nl * 128], ps[X][:, 0:nl * 128], AF.Exp, [bps[X]], [bE[X]], scale=0.125)
                    act(E[:, X, 384:640], ps[2 + X][:, 0:256], AF.Exp, [bps[2 + X]], [bE[X]], scale=0.125)
                    for li, (kb, mk) in enumerate(loc):
                        if mk is not None:
                            tt("dve", E[:, X, li * 128:(li + 1) * 128], E[:, X, li * 128:(li + 1) * 128], MW[:, mk, :], ALU.mult, [bE[X], bconst], [bE[X]])
                for X in range(2):
                    rows = slice(64 * X, 64 * X + 64)
                    g = 2 * m + X
                    n = nl + NCB
                    for li, (kb, _) in enumerate(loc if 'v' in L1F else []):
                        mm(ps[4][rows, 0:128], Vw[:, kb, g * 64:(g + 1) * 64], E[:, X, li * 128:(li + 1) * 128], li == 0, False, [bVw[kb], bE[X]], [bps[4]])
                    for cb in range(NCB if 'v' in L1F else 0):
                        mm(ps[4][rows, 0:128], Vc[:, cb, g * 64:(g + 1) * 64], E[:, X, 384 + cb * 128:384 + (cb + 1) * 128], False, cb == NCB - 1, [bctx, bE[X]], [bps[4]])
                    for li in range(nl if 'd' in L1F else 0):
                        mm(ps[6][rows, 0:128], onesb[:, 0:64], E[:, X, li * 128:(li + 1) * 128], li == 0, False, [bconst, bE[X]], [bps[6]])
                    for cb in range(NCB if 'd' in L1F else 0):
                        mm(ps[6][rows, 0:128], onesb[:, 0:64], E[:, X, 384 + cb * 128:384 + (cb + 1) * 128], False, cb == NCB - 1, [bconst, bE[X]], [bps[6]])
                if 'n' not in L1F:
                    continue
                ts("dve", rd, ps[6][:, 0:128], sinkE[:, j:j + 1], None, ALU.add, None, [bps[6], bconst], [brd])
                T.op("dve", lambda h: h.reciprocal(out=rd, in_=rd), [brd], [brd])
                tt("dve", attT[:, j, :], ps[4][:, 0:128], rd, ALU.mult, [bps[4], brd], [battT])
            if 'o' not in L1F:
                return
            for dh in range(2):
                for fc in range(8):
                    mm(ps[5 + 2 * dh][:, :], attT[:, fc, :], wout[:, fc, dh * 512:(dh + 1) * 512], fc == 0, fc == 7, [battT, bwout], [bps[5 + 2 * dh]])
            resid_ln(qb, ps[5][:, :], ps[7][:, :], bps[5], bps[7], gt[:, 0, :], bgt, gt, bgt, tmp, btmp, st6, mv, bst)

        for blk in range(NB + 1):
            if blk < NB:
                proj_block(blk)
            if blk >= 1:
                attn_block(blk - 1)
        T.barrier()

    if stop_after == "L1only":
        layer1_mixer()
    elif stop_after != "M":
        layer0_mixer()
        if stop_after != "L0A":
            mlp_phase(0, NSB)
            if stop_after != "L0":
                layer1_mixer()
                if stop_after != "L1A":
                    mlp_phase(1, NB)
    evs = []
    for blk in range(NB):
        evs.append(dma("sp", out[blk * 128:(blk + 1) * 128, :], xs[:, blk, :], [bxs[blk]], []))
    dbg = None
    if stop_after is not None:
        dbg = nc.dram_tensor("dbg", [128, 2 * 96 + 2 * D], F32, kind="ExternalOutput").ap()
        evs.append(dma("sp", dbg[:, 0:192], modT[:, :, :].rearrange("p a b -> p (a b)"), [bmodT], []))
        evs.append(dma("sp", dbg[:, 192:192 + D], xs[:, NB, :], [bxs[NB]], []))
        evs.append(dma("sp", dbg[:, 192 + D:192 + 2 * D], xs[:, NB + 1, :], [bxs[NB + 1]], []))
    T.wait_all_on("sp", evs)
    T.emit(st)
    st.close()
    return nc, T


def _consts():
    s = np.arange(128)[:, None]; c = np.arange(128)[None, :]
    same = (s // 64) == (c // 64)
    LC = np.zeros((128, 2, 130), np.float32)
    LC[:, 0, :128] = np.where(same & (s <= c), -1.0 / 16, 0.0)
    LC[:, 1, :128] = np.where(same & (s >= c), -1.0 / 16, 0.0)
    for d in range(2):
        LC[:64, d, 128] = -1.0 / 16
        LC[64:, d, 129] = -1.0 / 16
    MR = np.zeros((128, 2, 128), np.float32)
    MR[:, 0, :] = np.where(same & (s > c), -1.0 / 16, 0.0)
    MR[:, 1, :] = np.where(same & (s < c), -1.0 / 16, 0.0)
    Mf = np.where(same & (s <= c), 1.0, 0.0).astype(np.float32)
    Mb = np.where(same & (s > c), 1.0, 0.0).astype(np.float32)
    MK4 = np.stack([Mf, Mb, Mf, Mb], axis=1)
    MRF = np.zeros((128, 3, 128), np.float32)
    MRF[:, 0, :] = np.where(s > c, -1.0 / 16, 0.0)
    MRF[:, 1, :] = np.where(s < c, -1.0 / 16, 0.0)
    MRF[:, 2, :] = -1.0 / 16
    MW = np.stack([np.where(s >= c, 1.0, 0.0), np.where(s <= c, 1.0, 0.0)], axis=1).astype(np.float32)
    perm = np.zeros((128, 128), np.float32)
    for m in range(128):
        w = m % 32
        p = m + 16 if w < 16 else m - 16
        perm[p, m] = 1.0
    return dict(ident=np.eye(128, dtype=np.float32), LC=LC, MR=MR, MRF=MRF, MK4=np.ascontiguousarray(MK4), MW=np.ascontiguousarray(MW), perm=perm)


def _rope_tables(w0):
    r0 = w0 // 64
    P = np.zeros((128, 100), np.float32); I = np.zeros((128, 4), np.float32)
    for d in range(128):
        dd = d % 64
        sec = dd // 32; w = dd % 32
        if sec == 0:
            P[d, 0:36] = np.arange(r0, r0 + 36)
        else:
            P[d, 36:100] = np.arange(64)
        I[d, 0] = w % 16
        I[d, 1] = -1.0 if w < 16 else 1.0
        I[d, 2] = 1.0 if sec == 0 else 0.0
        I[d, 3] = 1.0 if sec == 1 else 0.0
    return P, I


_WIN_COLS = np.concatenate([np.arange(0, 256), np.arange(1536, 1792), np.arange(256, 512), np.arange(1792, 2048),
                            np.arange(512, 1024), np.arange(2048, 2560), np.arange(1024, 1536), np.arange(2560, 3072), np.arange(3072, 3104)])
_CACHE = {}


def _prep_inputs(inp):
    f = lambda a: np.ascontiguousarray(np.asarray(a, dtype=np.float32))
    x = f(inp["x"]); c = f(inp["c"]); ctx = f(inp["ctx"]); c_ctx = f(inp["c_ctx"])
    cst = _consts()
    order = []
    for j in range(8):
        m, i = j // 4, j % 4
        order += [(2 * m) * 4 + i, (2 * m + 1) * 4 + i]
    qcols = np.concatenate([np.arange(h * 64, (h + 1) * 64) for h in order])
    w_qkv = f(inp["od_w_qkv"])[0]
    w_qkv_p = np.ascontiguousarray(np.concatenate([w_qkv[:, qcols], w_qkv[:, 1024:]], axis=1))
    w_out1 = np.ascontiguousarray(f(inp["od_w_out"])[0][qcols, :])
    sink = f(inp["od_sink"])[0]
    sinkc = np.zeros((128, 8), np.float32)
    for j in range(8):
        sinkc[:64, j] = sink[order[2 * j]]; sinkc[64:, j] = sink[order[2 * j + 1]]
    th = f(inp["ev_ret_theta"])[0]
    theta = np.ascontiguousarray(np.broadcast_to(np.repeat(th, 64, axis=1)[None], (128, 2, 256))).astype(np.float32)
    gkw = f(inp["ev_gla_gk_w"])[0]; gkb = f(inp["ev_gla_gk_b"])[0]
    wg = np.zeros((33, 512), np.float32)
    wg[0:16, 0:256] = gkw[0]; wg[16:32, 256:512] = gkw[1]; wg[32, 0:256] = gkb[0]; wg[32, 256:512] = gkb[1]
    gng = f(inp["ev_gla_norm_g"])[0].reshape(128, 1)
    shared = dict(w_mod=f(inp["w_mod"]), b_mod=f(inp["b_mod"]), ln_g=f(inp["ln_g"]), ln_b=f(inp["ln_b"]),
                  mlp_w1=f(inp["mlp_w1"]), mlp_w2=f(inp["mlp_w2"]), w_in=np.ascontiguousarray(f(inp["ev_w_in"])[0][:, _WIN_COLS]), theta=theta, wg=wg, gng=np.ascontiguousarray(gng),
                  w_out0=f(inp["ev_w_out"])[0], w_qkv=w_qkv_p, sinkc=sinkc, w_out1=w_out1, **cst)
    maps = []
    offs = []
    for core in range(8):
        b, j = core // 4, core % 4
        w0 = min(max(j * 2048 - 128, 0), 8192 - WIN)
        offs.append(j * 2048 - w0)
        xw = x[b, w0:w0 + WIN]
        pre = x[b, :w0].reshape(-1, 128, D)[::-1].reshape(-1, D)
        xo = np.concatenate([pre, x[b, w0 + WIN:]], axis=0)
        mo = np.zeros((128, NOB, 2), np.float32)
        npre = w0 // 128
        mo[:, :npre, 0] = 1.0; mo[:, npre:, 1] = 1.0
        cv = np.stack([c[b], c_ctx], axis=0)
        cvT = np.ascontiguousarray(cv.reshape(2, 8, 128).transpose(2, 1, 0))
        rP, rI = _rope_tables(w0)
        m = dict(shared)
        m.update(xw=np.ascontiguousarray(xw), xo=np.ascontiguousarray(xo), xc=np.ascontiguousarray(ctx[b]), cvT=cvT, mo=mo, ropeP=rP, ropeI=rI)
        maps.append(m)
    return maps, offs


def kernel(**inputs):
    maps, offs = _prep_inputs(inputs)
    if "nc" not in _CACHE:
        _CACHE["nc"] = build_nc()[0]
    nc = _CACHE["nc"]
    res = run_bass_kernel_spmd(nc, maps, core_ids=list(range(8)))
    out = np.zeros((2, 8192, D), np.float32)
    for core in range(8):
        b, j = core // 4, core % 4
        o = res.results[core]["out"]
        out[b, j * 2048:(j + 1) * 2048] = o[offs[core]:offs[core] + 2048]
    return out
```
